# Optimizing a Trainium2 kernel written in Bass

```python
import jax, jax.numpy as jnp
from jax import lax
import numpy as np

D_MODEL = 1024
BATCH = 16
SEQ = 4096
DEPTH = 2
DEC_BATCH = 8
DEC_SEQ = 64
PAST_LEN = 2048

CHUNK = 64
D_MIX = D_MODEL
D_CONV = D_MIX // 2
D_RWKV = D_MIX - D_CONV
HEAD_DIM = 64
N_HEADS = D_RWKV // HEAD_DIM
CONV_WIDTH = 31
CONV_BUF = min(CONV_WIDTH - 1, PAST_LEN)
LORA_W = 64
LORA_A = 64
LORA_G = 128
D_SHIFT = 3 * D_RWKV + LORA_W + LORA_A + LORA_G
D_IN = 2 * D_CONV + D_SHIFT
D_FF = -(-8 * D_MODEL // (3 * 256)) * 256
RWKV_SPLIT = (D_RWKV, 2 * D_RWKV, 3 * D_RWKV, 3 * D_RWKV + LORA_W, 3 * D_RWKV + LORA_W + LORA_A)
RMS_EPS = 1e-6
LN_EPS = 1e-5
GN_EPS = 64e-5
L2_EPS = 1e-12

kernel_name = 'hymba_conformer_rwkv7_stream_step'


def _rms_norm(x, g):
    xf = x.astype(jnp.float32)
    y = xf * lax.rsqrt(jnp.mean(xf * xf, axis=-1, keepdims=True) + RMS_EPS)
    return (y * g.astype(jnp.float32)).astype(x.dtype)


def _layer_norm(x, g, b):
    xf = x.astype(jnp.float32)
    m = jnp.mean(xf, axis=-1, keepdims=True)
    v = jnp.mean(jnp.square(xf - m), axis=-1, keepdims=True)
    y = (xf - m) * lax.rsqrt(v + LN_EPS) * g.astype(jnp.float32) + b.astype(jnp.float32)
    return y.astype(x.dtype)


def _conv_mix(u, conv_buf, conv_dw, conv_b, ln_g, ln_b):
    glu = u[..., :D_CONV] * jax.nn.sigmoid(u[..., D_CONV:])
    full = jnp.concatenate([conv_buf.astype(glu.dtype), glu], axis=1)
    y = lax.conv_general_dilated(full, conv_dw[:, None, :].astype(full.dtype), window_strides=(1,),
                                 padding='VALID', dimension_numbers=('NWC', 'WIO', 'NWC'),
                                 feature_group_count=D_CONV) + conv_b
    y = _layer_norm(y, ln_g, ln_b)
    return y * jax.nn.sigmoid(y), full[:, -CONV_BUF:]


def _wkv_step(S, inp):
    r, w, k, v, kk, kka = inp
    sa = jnp.einsum('bhvk,bhk->bhv', S, -kk)
    S = S * w[:, :, None, :] + sa[..., None] * kka[:, :, None, :] + v[..., None] * k[:, :, None, :]
    return S, jnp.einsum('bhvk,bhk->bhv', S, r)


def _rwkv_mix(z, shift_prev, wkv0, mu_shift, w0, w2, a0, a2, g2, k_k, k_a, r_k, lnx_g, lnx_b):
    B, T, _ = z.shape
    f32 = jnp.float32
    z_prev = jnp.concatenate([shift_prev[:, None, :].astype(z.dtype), z[:, :-1]], axis=1)
    zm = z + (z_prev - z) * mu_shift
    zr, zk, zv, zw, za, zg = jnp.split(zm, RWKV_SPLIT, axis=-1)
    heads = lambda t: t.astype(f32).reshape(B, T, N_HEADS, HEAD_DIM)
    w_log = -jax.nn.softplus(-(w0 + jnp.tanh(zw) @ w2).astype(f32)) - 0.5
    decay = heads(jnp.exp(-jnp.exp(w_log)))
    a = heads(jax.nn.sigmoid((a0 + za @ a2).astype(f32)))
    g = jax.nn.sigmoid(zg) @ g2
    r, k, v = heads(zr), heads(zk), heads(zv)
    kk = k * k_k.astype(f32).reshape(N_HEADS, HEAD_DIM)
    kk = kk * lax.rsqrt(jnp.sum(kk * kk, axis=-1, keepdims=True) + L2_EPS)
    k = k * (1.0 + (a - 1.0) * k_a.astype(f32).reshape(N_HEADS, HEAD_DIM))
    tm = lambda t: jnp.moveaxis(t, 1, 0)
    S_fin, ys = lax.scan(_wkv_step, wkv0.astype(f32), (tm(r), tm(decay), tm(k), tm(v), tm(kk), tm(kk * a)))
    y = jnp.moveaxis(ys, 0, 1)
    m = jnp.mean(y, axis=-1, keepdims=True)
    var = jnp.mean(jnp.square(y - m), axis=-1, keepdims=True)
    y = ((y - m) * lax.rsqrt(var + GN_EPS)).reshape(B, T, D_RWKV) * lnx_g + lnx_b
    bonus = (jnp.sum(r * k * r_k.astype(f32), axis=-1, keepdims=True) * v).reshape(B, T, D_RWKV)
    out = ((y + bonus) * g.astype(f32)).astype(z.dtype)
    return out, z[:, -1], S_fin.astype(wkv0.dtype)


def _layer(x, c, conv_buf, shift_prev, wkv, w_mod, b_mod, g_mix_pre, g_mix_post, g_ffn_pre, g_ffn_post,
           w_in, conv_dw, conv_b, conv_ln_g, conv_ln_b, mu_shift, w0, w2, a0, a2, g2, k_k, k_a, r_k,
           lnx_g, lnx_b, w_out, w_gate, w_up, w_down):
    mod = jax.nn.silu(c) @ w_mod + b_mod
    sh1, sc1, ga1, sh2, sc2, ga2 = [m[:, None, :] for m in jnp.split(mod, 6, axis=-1)]
    h = _rms_norm(x, g_mix_pre) * (1.0 + sc1) + sh1
    u = h @ w_in
    conv_out, new_buf = _conv_mix(u[..., :2 * D_CONV], conv_buf, conv_dw, conv_b, conv_ln_g, conv_ln_b)
    rwkv_out, new_shift, new_wkv = _rwkv_mix(u[..., 2 * D_CONV:], shift_prev, wkv, mu_shift, w0, w2,
                                             a0, a2, g2, k_k, k_a, r_k, lnx_g, lnx_b)
    mix = jnp.concatenate([conv_out, rwkv_out], axis=-1) @ w_out
    x = x + (1.0 + ga1) * _rms_norm(mix, g_mix_post)
    h = _rms_norm(x, g_ffn_pre) * (1.0 + sc2) + sh2
    f = (jax.nn.silu(h @ w_gate) * (h @ w_up)) @ w_down
    x = x + (1.0 + ga2) * _rms_norm(f, g_ffn_post)
    return x, new_buf, new_shift, new_wkv


def setup_inputs(seed: int = 0) -> dict:
    key = jax.random.key(seed)
    ks = iter(jax.random.split(key, 48))
    f32 = jnp.float32
    nrm = lambda shape, s: jax.random.normal(next(ks), shape, f32) * s
    L = DEPTH
    return {
        'x_prompt': nrm((BATCH, SEQ, D_MODEL), 1.0),
        'x_sample': nrm((DEC_BATCH, DEC_SEQ, D_MODEL), 1.0),
        'cache_conv': nrm((L, DEC_BATCH, CONV_BUF, D_CONV), 0.5),
        'state_shift': nrm((L, DEC_BATCH, D_SHIFT), 1.0),
        'state_wkv': nrm((L, DEC_BATCH, N_HEADS, HEAD_DIM, HEAD_DIM), 0.3),
        'c_prompt': nrm((BATCH, D_MODEL), 1.0),
        'c_sample': nrm((DEC_BATCH, D_MODEL), 1.0),
        'w_mod': nrm((L, D_MODEL, 6 * D_MODEL), 0.1 * D_MODEL ** -0.5),
        'b_mod': nrm((L, 6 * D_MODEL), 0.01),
        'g_mix_pre': 1.0 + nrm((L, D_MODEL), 0.05),
        'g_mix_post': 1.0 + nrm((L, D_MODEL), 0.05),
        'g_ffn_pre': 1.0 + nrm((L, D_MODEL), 0.05),
        'g_ffn_post': 1.0 + nrm((L, D_MODEL), 0.05),
        'w_in': nrm((L, D_MODEL, D_IN), D_MODEL ** -0.5),
        'conv_dw': nrm((L, CONV_WIDTH, D_CONV), CONV_WIDTH ** -0.5),
        'conv_b': nrm((L, D_CONV), 0.01),
        'conv_ln_g': 1.0 + nrm((L, D_CONV), 0.05),
        'conv_ln_b': nrm((L, D_CONV), 0.01),
        'mu_shift': jax.random.uniform(next(ks), (L, D_SHIFT), f32),
        'w0': jax.random.uniform(next(ks), (L, D_RWKV), f32, -5.0, 1.0),
        'w2': nrm((L, LORA_W, D_RWKV), 0.5 * LORA_W ** -0.5),
        'a0': nrm((L, D_RWKV), 0.3),
        'a2': nrm((L, LORA_A, D_RWKV), 0.5 * LORA_A ** -0.5),
        'g2': nrm((L, LORA_G, D_RWKV), LORA_G ** -0.5),
        'k_k': 0.85 + nrm((L, D_RWKV), 0.05),
        'k_a': 1.0 + nrm((L, D_RWKV), 0.05),
        'r_k': nrm((L, N_HEADS, HEAD_DIM), 0.1),
        'lnx_g': 1.0 + nrm((L, D_RWKV), 0.05),
        'lnx_b': nrm((L, D_RWKV), 0.01),
        'w_out': nrm((L, D_MIX, D_MODEL), D_MIX ** -0.5),
        'w_gate': nrm((L, D_MODEL, D_FF), D_MODEL ** -0.5),
        'w_up': nrm((L, D_MODEL, D_FF), D_MODEL ** -0.5),
        'w_down': nrm((L, D_FF, D_MODEL), D_FF ** -0.5),
    }


def reference(x_prompt, x_sample, cache_conv, state_shift, state_wkv, c_prompt, c_sample,
              w_mod, b_mod, g_mix_pre, g_mix_post, g_ffn_pre, g_ffn_post, w_in, conv_dw, conv_b,
              conv_ln_g, conv_ln_b, mu_shift, w0, w2, a0, a2, g2, k_k, k_a, r_k, lnx_g, lnx_b,
              w_out, w_gate, w_up, w_down):
    weights = (w_mod, b_mod, g_mix_pre, g_mix_post, g_ffn_pre, g_ffn_post, w_in, conv_dw, conv_b,
               conv_ln_g, conv_ln_b, mu_shift, w0, w2, a0, a2, g2, k_k, k_a, r_k, lnx_g, lnx_b,
               w_out, w_gate, w_up, w_down)
    bp = x_prompt.shape[0]
    dt = x_prompt.dtype
    zero_conv = jnp.zeros((bp, CONV_BUF, D_CONV), dt)
    zero_shift = jnp.zeros((bp, D_SHIFT), dt)
    zero_wkv = jnp.zeros((bp, N_HEADS, HEAD_DIM, HEAD_DIM), dt)
    yp, ys = x_prompt, x_sample
    cp, sp, wp, cs, ss, wsm = [], [], [], [], [], []
    for l in range(DEPTH):
        lw = [w[l] for w in weights]
        yp, b1, s1, k1 = _layer(yp, c_prompt, zero_conv, zero_shift, zero_wkv, *lw)
        ys, b2, s2, k2 = _layer(ys, c_sample, cache_conv[l], state_shift[l], state_wkv[l], *lw)
        cp.append(b1); sp.append(s1); wp.append(k1)
        cs.append(b2); ss.append(s2); wsm.append(k2)
    conv_prompt, shift_prompt, wkv_prompt = jnp.stack(cp), jnp.stack(sp), jnp.stack(wp)
    conv_sample, shift_sample, wkv_sample = jnp.stack(cs), jnp.stack(ss), jnp.stack(wsm)
    return (yp, ys, conv_prompt, shift_prompt, wkv_prompt, conv_sample, shift_sample, wkv_sample)
```

```python
import contextlib
import numpy as np
import concourse.bass as bass
import concourse.mybir as mybir
from concourse.bass_utils import run_bass_kernel_spmd

F32 = mybir.dt.float32
BF16 = mybir.dt.bfloat16
ALU = mybir.AluOpType
AF = mybir.ActivationFunctionType
AX = mybir.AxisListType

D = 1024; DC = 512; DR = 512; NH = 8; HD = 64; CW = 31; CB = 30
DSH = 1792; DIN = 2816; DFF = 2816; L = 2
RMS_EPS = 1e-6; LN_EPS = 1e-5; GN_EPS = 64e-5; L2_EPS = 1e-12
EM05 = float(np.exp(-0.5))
TT = 256
DEBUG = False
NCORES = 8

VOFF = {}
_o = 0
for _n, _k in [("g_mix_pre", 8), ("g_mix_post", 8), ("g_ffn_pre", 8), ("g_ffn_post", 8), ("b_mod", 48),
               ("conv_b", 4), ("conv_ln_g", 4), ("conv_ln_b", 4), ("mu", 14), ("w0", 4), ("a0", 4), ("k_k", 4),
               ("k_a", 4), ("r_k", 4), ("lnx_g", 4), ("lnx_b", 4), ("conv_dw", 124),
               ("omu", 14), ("hw0", 4), ("ha0", 4), ("omka", 4), ("hlg", 4), ("hlb", 4)]:
    VOFF[_n] = _o; _o += _k
NV = _o
NV_IN = VOFF["omu"]


class Dep:
    __slots__ = ("w", "r")

    def __init__(self):
        self.w = None; self.r = {}


class T:
    def __init__(self, t, nd=1):
        self.t = t; self.ds = [Dep() for _ in range(nd)]

    def __getitem__(self, k):
        return self.t[k]

    @property
    def d(self):
        return self.ds[0]


def bc(ap, pos, n):
    dims = [list(x) for x in ap.ap]
    dims.insert(pos, [0, n])
    return bass.AP(ap.tensor, ap.offset, dims)


class Builder:
    def __init__(self, nc, es):
        self.nc = nc; self.es = es
        self.eng = {"pe": nc.tensor, "act": nc.scalar, "dve": nc.vector, "pool": nc.gpsimd, "sp": nc.sync}
        self.sem = {e: es.enter_context(nc.semaphore("s_" + e)) for e in self.eng}
        self.cnt = {e: 0 for e in self.eng}
        self.known = {e: {} for e in self.eng}
        self.dcnt = {}
        self.uid = 0
        self.rec = None
        self.bankdeps = set()

    def name(self, p):
        self.uid += 1
        return "%s%d" % (p, self.uid)

    def sb(self, shape, dt=F32, nd=1, es=None, name="t"):
        return T((es or self.es).enter_context(self.nc.sbuf_tensor(self.name(name), list(shape), dt)), nd)

    def dslot(self, name):
        k = self.name("dq_" + name)
        self.sem[k] = self.es.enter_context(self.nc.semaphore(k)); self.dcnt[k] = 0
        return k

    def _wait(self, e, key, val, raw=False):
        if key == e and (e == "pe" or not raw):
            return
        if self.known[e].get(key, 0) >= val:
            return
        self.eng[e].wait_ge(self.sem[key], val); self.known[e][key] = val

    def _split(self, e, R, W):
        if e == "pe" or not self.bankdeps: return R, W, ()
        X = [d for d in W if id(d) in self.bankdeps]
        if not X: return R, W, ()
        return R, [d for d in W if id(d) not in self.bankdeps], X

    def _deps(self, e, R, W, X=(), rmw=True):
        for d in R:
            if d.w: self._wait(e, *d.w, raw=True)
        for d in W:
            if d.w: self._wait(e, *d.w, raw=rmw)
            for k, v in d.r.items(): self._wait(e, k, v)
        for d in X:
            if d.w: self._wait(e, *d.w, raw=True)
            for k, v in d.r.items():
                if k != e: self._wait(e, k, v)

    def _mark(self, key, val, R, W, X=()):
        for d in R: d.r[key] = val
        for d in X: d.r[key] = val
        for d in W: d.w = (key, val); d.r = {}

    def record(self, body):
        self.rec = []
        body()
        r, self.rec = self.rec, None
        return r

    def replay(self, *streams):
        n = max(len(x) for x in streams)
        for i in range(n):
            for x in streams:
                if i < len(x): self.op(*x[i])

    def op(self, e, fn, R=(), W=(), rmw=True):
        if self.rec is not None:
            self.rec.append((e, fn, tuple(R), tuple(W), rmw)); return
        R, W, X = self._split(e, R, W)
        self._deps(e, R, W, X, rmw)
        ins = fn(self.eng[e])
        self.cnt[e] += 1
        ins.then_inc(self.sem[e], 1)
        self._mark(e, self.cnt[e], R, W, X)

    def dma(self, q, slot, out, in_, R=(), W=(), **kw):
        assert self.rec is None
        if q == "pool": slot = self.dslot("sw")
        if self.dcnt[slot]: self._wait(q, slot, self.dcnt[slot], raw=True)
        self._deps(q, R, W)
        ins = self.eng[q].dma_start(out=out, in_=in_, **kw)
        self.dcnt[slot] += 16
        ins.then_inc(self.sem[slot], 16)
        self._mark(slot, self.dcnt[slot], R, W)

    def barrier(self):
        for e in self.eng:
            for o in self.eng:
                if o != e and self.cnt[o]: self._wait(e, o, self.cnt[o])
            for k, v in self.dcnt.items():
                if v: self._wait(e, k, v)

    def mm(self, out, lhsT, rhs, start=True, stop=True, R=(), W=(), skip=False):
        self.op("pe", lambda e: e.matmul(out, lhsT=lhsT, rhs=rhs, start=start, stop=stop, skip_group_check=skip), R, W)

    def tr(self, out, in_, ident, R=(), W=()):
        self.op("pe", lambda e: e.transpose(out=out, in_=in_, identity=ident), R, W)

    @staticmethod
    def _same(out, *ins):
        n = out.tensor.name
        return any(hasattr(i, "tensor") and i.tensor.name == n for i in ins)

    def act(self, out, in_, func, scale=1.0, bias=0.0, R=(), W=()):
        self.op("act", lambda e: e.activation(out=out, in_=in_, func=func, scale=scale, bias=bias), R, W, self._same(out, in_, scale, bias))

    def tt(self, eng, out, in0, in1, op, R=(), W=()):
        self.op(eng, lambda e: e.tensor_tensor(out=out, in0=in0, in1=in1, op=op), R, W, self._same(out, in0, in1))

    def ts(self, eng, out, in0, s1, s2, op0, op1=None, R=(), W=()):
        rmw = self._same(out, in0, s1, s2)
        if op1 is None:
            self.op(eng, lambda e: e.tensor_scalar(out=out, in0=in0, scalar1=s1, scalar2=None, op0=op0), R, W, rmw)
        else:
            self.op(eng, lambda e: e.tensor_scalar(out=out, in0=in0, scalar1=s1, scalar2=s2, op0=op0, op1=op1), R, W, rmw)

    def stt(self, out, in0, scalar, in1, op0, op1, R=(), W=()):
        self.op("dve", lambda e: e.scalar_tensor_tensor(out=out, in0=in0, scalar=scalar, in1=in1, op0=op0, op1=op1), R, W, self._same(out, in0, scalar, in1))

    def cp(self, eng, out, in_, R=(), W=()):
        rmw = self._same(out, in_)
        if eng == "act":
            self.op("act", lambda e: e.copy(out=out, in_=in_), R, W, rmw)
        else:
            self.op(eng, lambda e: e.tensor_copy(out=out, in_=in_), R, W, rmw)


def build(SEQ, learn=False):
    NPASS = SEQ // TT
    nc = bass.Bass("TRN2", target_bir_lowering=False)
    din = lambda n, s: nc.dram_tensor(n, list(s), F32, kind="ExternalInput").ap()
    dout = lambda n, s: nc.dram_tensor(n, list(s), F32, kind="ExternalOutput").ap()
    xp = din("xp", [2, SEQ, D]); xs = din("xs", [64, D]); cT = din("cT", [128, 8, 3])
    cconv = din("cconv", [L, CB, DC]); sshift = din("sshift", [L, 14, 128]); swkv = din("swkv", [L, NH, HD, HD])
    wmod = din("wmod", [L, D, 6 * D]); vecs = din("vecs", [L, 128, NV_IN])
    wa2 = din("wa2", [L, 128, DR]); g2 = din("g2", [L, 128, DR])
    WSPEC = [("w_in", 22, 1024), ("w_out", 8, 1024), ("w_gate", 22, 1024), ("w_up", 22, 1024), ("w_down", 8, 2816)]
    wsrc = {n: din(n, [L, ns, 128, wd]) for n, ns, wd in WSPEC}
    wscr = {n: nc.dram_tensor("scr_" + n, [L, ns, 128, wd], BF16).ap() for n, ns, wd in WSPEC}
    yp = dout("yp", [2, SEQ, D]); ys = dout("ys", [64, D])
    o_conv = dout("o_conv", [L, 3, CB, DC]); o_shift = dout("o_shift", [L, 3, 14, 128]); o_wkv = dout("o_wkv", [L, 3, NH, HD, HD])

    with contextlib.ExitStack() as es:
        K = Builder(nc, es)
        sb = K.sb
        banks = [T(es.enter_context(nc.psum_tensor("bank%d" % i, [128, 512], F32))) for i in range(8)]
        for b in banks: b.open = False
        K.bankdeps = {id(b.d) for b in banks}
        rings = {"all": [0, 1, 2, 3, 4, 7], "A": [0, 1, 2], "B": [3, 4, 7]}; rposd = {"all": 0, "A": 0, "B": 0}; cur_ring = ["all"]

        def bank():
            ring = rings[cur_ring[0]]
            for _ in range(len(ring)):
                b = banks[ring[rposd[cur_ring[0]] % len(ring)]]; rposd[cur_ring[0]] += 1
                if not b.open: break
            assert not b.open, "all psum ring banks are open"
            b.open = True
            return b

        def rel(b):
            b.open = False

        identf = sb([128, 128]); ident = sb([128, 128], BF16); ones = sb([128, 128], BF16); blk = sb([128, 128], BF16)
        onesf = sb([128, 128])
        mts = sb([128, 128], BF16); ms = sb([128, 128], BF16); mti = sb([128, 128], BF16); scr_f = sb([128, 128])
        rmask = sb([128, 512])
        cst = Dep()

        def pm(fn): K.op("pool", fn, W=[cst])
        pm(lambda e: e.memset(identf[:, :], 0.0))
        pm(lambda e: e.affine_select(out=identf[:, :], in_=identf[:, :], pattern=[[-1, 128]], compare_op=ALU.not_equal, fill=1.0, base=0, channel_multiplier=1))
        pm(lambda e: e.tensor_copy(out=ident[:, :], in_=identf[:, :]))
        pm(lambda e: e.memset(ones[:, :], 1.0))
        pm(lambda e: e.memset(onesf[:, :], 1.0))
        pm(lambda e: e.memset(blk[:, :], 0.0))
        pm(lambda e: e.memset(blk[0:64, 0:64], 1.0))
        pm(lambda e: e.memset(blk[64:128, 64:128], 1.0))
        for m, (pat, cm, cmp) in [(mts, ([[1, 128]], -1, ALU.is_gt)), (ms, ([[-1, 128]], 1, ALU.is_gt)), (mti, ([[1, 128]], -1, ALU.is_ge))]:
            pm(lambda e: e.memset(scr_f[:, :], 1.0))
            pm(lambda e, pat=pat, cm=cm, cmp=cmp: e.affine_select(out=scr_f[:, :], in_=scr_f[:, :], pattern=pat, compare_op=cmp, fill=0.0, base=0, channel_multiplier=cm))
            pm(lambda e, m=m: e.tensor_copy(out=m[:, :], in_=scr_f[:, :]))
        pm(lambda e: e.memset(rmask[:, :], 1.0))
        pm(lambda e: e.memset(rmask[:, :].rearrange("p (c t) -> p c t", t=128)[:, :, 0:1], 0.0))

        dbgt = sb([128, 512] if DEBUG else [128, 2]); dbgq = K.dslot("dbg"); dbg_on = [False]

        def dbg(name, ap, R=()):
            if not (DEBUG and dbg_on[0]): return
            shp = list(ap.shape); p = shp[0]; n = int(np.prod(shp[1:]))
            o = nc.dram_tensor("dbg_" + name, [p, n], F32, kind="ExternalOutput").ap()
            dv = dbgt[0:p, 0:n]
            if len(shp) == 3: dv = dv.rearrange("p (a b) -> p a b", b=shp[2])
            K.cp("dve", dv, ap, R=list(R), W=[dbgt.d])
            K.dma("sp", dbgq, o, dbgt[0:p, 0:n], R=[dbgt.d])

        q_misc = K.dslot("misc")
        vec = [sb([128, NV]) for _ in range(L)]
        wa2b = [sb([128, DR], BF16) for _ in range(L)]; g2b = [sb([128, DR], BF16) for _ in range(L)]
        modv = [sb([128, 6, 8, 3]) for _ in range(L)]
        for l in range(L):
            K.dma("sp", q_misc, vec[l][:, 0:NV_IN], vecs[l], W=[vec[l].d])
            K.dma("pool", q_misc, wa2b[l][:, :], wa2[l], W=[wa2b[l].d])
            K.dma("pool", q_misc, g2b[l][:, :], g2[l], W=[g2b[l].d])
            V = vec[l]
            vc = lambda n, a=0, k=1: V[:, VOFF[n] + a:VOFF[n] + a + k]
            K.ts("dve", vc("omu", 0, 14), vc("mu", 0, 14), -1.0, 1.0, ALU.mult, ALU.add, W=[V.d])
            K.ts("dve", vc("hw0", 0, 4), vc("w0", 0, 4), 0.5, None, ALU.mult, W=[V.d])
            K.ts("dve", vc("ha0", 0, 4), vc("a0", 0, 4), 0.5, None, ALU.mult, W=[V.d])
            K.ts("dve", vc("omka", 0, 4), vc("k_a", 0, 4), -1.0, 1.0, ALU.mult, ALU.add, W=[V.d])
            K.ts("dve", vc("hlg", 0, 4), vc("conv_ln_g", 0, 4), 0.5, None, ALU.mult, W=[V.d])
            K.ts("dve", vc("hlb", 0, 4), vc("conv_ln_b", 0, 4), 0.5, None, ALU.mult, W=[V.d])

        with contextlib.ExitStack() as ph:
            cf = sb([128, 8, 3], es=ph); ct = sb([128, 8, 3], es=ph); scb = sb([128, 8, 3], BF16, es=ph)
            wm = [sb([128, 6 * D], BF16, es=ph) for _ in range(2)]
            q_wm = [K.dslot("wm0"), K.dslot("wm1")]
            K.dma("sp", q_misc, cf[:, :, :], cT, W=[cf.d])
            K.act(ct[:, :, :], cf[:, :, :], AF.Tanh, scale=0.5, R=[cf.d], W=[ct.d])
            K.ts("dve", ct[:, :, :], ct[:, :, :], 0.5, 0.5, ALU.mult, ALU.add, W=[ct.d])
            K.tt("dve", scb[:, :, :], ct[:, :, :], cf[:, :, :], ALU.mult, R=[cf.d, ct.d], W=[scb.d])
            i = 0
            for l in range(L):
                b = bank()
                for kc in range(8):
                    w = wm[i % 2]
                    K.dma("pool", q_wm[i % 2], w[:, :], wmod[l, kc * 128:(kc + 1) * 128, :], W=[w.d], max_dma_last_dim=4096)
                    for n in range(48):
                        K.mm(b.t[:, n * 3:n * 3 + 3], w[:, n * 128:(n + 1) * 128], scb[:, kc, :], start=(kc == 0 and n == 0), stop=(kc == 7),
                             R=[w.d, scb.d], W=[b.d], skip=True)
                    i += 1
                V = vec[l]; M = modv[l]
                K.tt("dve", M[:, :, :, :].rearrange("p a c s -> p (a c) s"), b.t[:, 0:144].rearrange("p (n s) -> p n s", s=3),
                     bc(V[:, VOFF["b_mod"]:VOFF["b_mod"] + 48], 2, 3), ALU.add, R=[V.d], W=[b.d, M.d])
                rel(b)
                for gi, gn in [(1, "g_mix_pre"), (2, "g_mix_post"), (4, "g_ffn_pre"), (5, "g_ffn_post")]:
                    K.stt(M[:, gi, :, :], M[:, gi, :, :], 1.0, bc(V[:, VOFF[gn]:VOFF[gn] + 8], 2, 3), ALU.add, ALU.mult, R=[V.d], W=[M.d])
            if DEBUG:
                dbg_on[0] = True
                for nm_, t_ in [("blk", blk), ("mts", mts), ("ms", ms), ("mti", mti), ("ident", ident)]: dbg(nm_, t_[:, :], R=[cst])
                dbg("modv", modv[0][:, :, :, :].rearrange("p a c s -> p (a c s)"), R=[modv[0].d]); dbg_on[0] = False
            K.barrier()

        scr_dep = {}
        for l in range(L):
            for n, ns, wd in WSPEC:
                d = Dep(); scr_dep[(n, l)] = d; q_cv = K.dslot("cv")
                if wd == 1024:
                    K.dma("pool", q_cv, wscr[n][l].rearrange("s p w -> (s p) w"), wsrc[n][l].rearrange("s p w -> (s p) w"), W=[d])
                else:
                    K.dma("pool", q_cv, wscr[n][l].rearrange("s p (h w) -> (s p h) w", h=2),
                          wsrc[n][l].rearrange("s p (h w) -> (s p h) w", h=2), W=[d])

        x = sb([128, 8, 512], nd=8)
        hT = sb([128, 8, 512], BF16, nd=8); mixT = sb([128, 8, 512], BF16, nd=8); rstd = sb([128, 512])
        stage = sb([128, 1024]); q_stage = K.dslot("stage")
        class Set: pass
        PS = Set(); PS.W = 2 * TT; PS.segs = [(0, TT, 0), (TT, TT, 1)]; PS.ns = 2; PS.tt = TT
        SS = Set(); SS.W = 64; SS.segs = [(0, 64, 2)]; SS.ns = 1; SS.tt = 64
        for S in (PS, SS):
            S.full = [sb([128, 4, S.ns, CB + S.tt], BF16, nd=4) for _ in range(L)]
            S.zh = [sb([128, 14, S.ns], nd=14) for _ in range(L)]
            S.ST = [sb([128, 4, S.ns, HD], nd=4) for _ in range(L)]; S.STb = [sb([128, 4, S.ns, 2, HD], BF16) for _ in range(L)]
            S.gl = sb([128, 4, S.ns, CB]); S.zl = sb([128, 14, S.ns])
        st = Dep()
        for l in range(L):
            K.op("pool", lambda e: e.memset(PS.full[l][:, :, :, 0:CB], 0.0), W=PS.full[l].ds)
            K.op("pool", lambda e: e.memset(PS.zh[l][:, :, :], 0.0), W=PS.zh[l].ds)
            K.op("pool", lambda e: e.memset(PS.ST[l][:, :, :, :], 0.0), W=PS.ST[l].ds)
            K.op("pool", lambda e: e.memset(PS.STb[l][:, :, :, :, :], 0.0), W=[PS.STb[l].d])
            K.op("pool", lambda e: e.memset(SS.STb[l][:, :, :, :, :], 0.0), W=[SS.STb[l].d])
            K.dma("sp", q_stage, stage[0:CB, 0:DC], cconv[l], W=[stage.d])
            b = bank()
            for c in range(4):
                K.tr(b.t[:, c * 32:c * 32 + CB], stage[0:CB, c * 128:(c + 1) * 128], identf[0:CB, 0:CB], R=[stage.d, cst], W=[b.d])
            K.cp("dve", SS.full[l][:, :, 0, 0:CB], b.t[:, 0:128].rearrange("p (c t) -> p c t", t=32)[:, :, 0:CB], W=[b.d] + SS.full[l].ds); rel(b)
            K.dma("sp", q_stage, stage[0:14, 0:128], sshift[l], W=[stage.d])
            b = bank()
            K.tr(b.t[:, 0:14], stage[0:14, 0:128], identf[0:14, 0:14], R=[stage.d, cst], W=[b.d])
            K.tt("dve", SS.zh[l][:, :, 0], b.t[:, 0:14], vec[l][:, VOFF["mu"]:VOFF["mu"] + 14], ALU.mult, R=[vec[l].d], W=[b.d] + SS.zh[l].ds); rel(b)
            K.dma("sp", q_stage, stage[0:64, 0:512].rearrange("v (h k) -> v h k", k=64), swkv[l].rearrange("h v k -> v h k"), W=[stage.d])
            b = bank()
            for c in range(4):
                K.tr(b.t[:, c * 64:(c + 1) * 64], stage[0:64, c * 128:(c + 1) * 128], identf[0:64, 0:64], R=[stage.d, cst], W=[b.d])
            K.cp("dve", SS.ST[l][:, :, 0, :], b.t[:, 0:256].rearrange("p (c v) -> p c v", v=64), W=[b.d] + SS.ST[l].ds)
            K.cp("act", SS.STb[l][0:64, :, 0, 0, :], b.t[0:64, 0:256].rearrange("p (c v) -> p c v", v=64), W=[b.d, SS.STb[l].d])
            K.cp("act", SS.STb[l][64:128, :, 0, 1, :], b.t[64:128, 0:256].rearrange("p (c v) -> p c v", v=64), W=[b.d, SS.STb[l].d]); rel(b)

        NR = 8
        slabs = [sb([128, 8, 128], BF16) for _ in range(NR)]; slabq = [K.dslot("sl%d" % i) for i in range(NR)]
        dslabs = [sb([128, 22, 128], BF16) for _ in range(2)]; dslabq = [K.dslot("dsl%d" % i) for i in range(2)]
        passes = ["s"] + list(range(NPASS))
        sched = []; dsched = []
        for _ in ([] if learn else passes):
            for l in range(L):
                sched += [("w_in", l, j) for j in WIN_ORDER] + [("w_out", l, j) for j in range(8)]
                for j in range(22): sched += [("w_gate", l, j), ("w_up", l, j)]
                dsched += [("w_down", l, j) for j in range(8)]
        sp_ = [0, 0]; dp_ = [0, 0]

        def next_slab(expect):
            if learn: sched.append(expect)
            i = sp_[1]; assert sched[i] == expect, (sched[i], expect)
            while sp_[0] < min(len(sched), i + NR):
                n, l, j = sched[sp_[0]]; s = slabs[sp_[0] % NR]
                K.dma("sp", slabq[sp_[0] % NR], s[:, :, :].rearrange("p k c -> p (k c)"), wscr[n][l, j], R=[scr_dep[(n, l)]], W=[s.d])
                sp_[0] += 1
            sp_[1] += 1
            return slabs[i % NR]

        def next_dslab(expect):
            if learn: dsched.append(expect)
            i = dp_[1]; assert dsched[i] == expect
            while dp_[0] < min(len(dsched), i + 2):
                n, l, j = dsched[dp_[0]]; s = dslabs[dp_[0] % 2]
                K.dma("sp", dslabq[dp_[0] % 2], s[:, :, :].rearrange("p k c -> p (k c)"), wscr[n][l, j], R=[scr_dep[(n, l)]], W=[s.d])
                dp_[0] += 1
            dp_[1] += 1
            return dslabs[i % 2]

        def phase_bufs(ph):
            if not hasattr(ph, "sq"):
                ph.sq = sb([128, 8, 512], BF16, es=ph, nd=8); ph.tmp = [sb([128, 256], es=ph) for _ in range(3)]
            return ph.sq

        def rms_stats(S, src, ph):
            W = S.W
            sq = phase_bufs(ph)
            b = bank()
            for c in range(8):
                if src is not None:
                    K.act(sq[:, c, 0:W], src[:, c, 0:W], AF.Square, R=[src.ds[c]], W=[sq.ds[c]])
                K.mm(b.t[:, 0:W], ones[:, :], sq[:, c, 0:W], start=(c == 0), stop=(c == 7), R=[sq.ds[c], cst], W=[b.d])
            K.act(rstd[:, 0:W], b.t[:, 0:W], AF.Ln, scale=1.0 / D, bias=RMS_EPS, W=[b.d, rstd.d]); rel(b)
            K.act(rstd[:, 0:W], rstd[:, 0:W], AF.Exp, scale=-0.5, W=[rstd.d])

        def norm_mod(S, l, gi, si, ph):
            rms_stats(S, x, ph)
            tmp = ph.tmp
            i = 0
            for (c0, n, s) in S.segs:
                for c in range(8):
                    t = tmp[i % 3]; i += 1
                    K.tt("dve", t[:, 0:n], x[:, c, c0:c0 + n], rstd[:, c0:c0 + n], ALU.mult, R=[x.ds[c], rstd.d], W=[t.d])
                    K.act(hT[:, c, c0:c0 + n], t[:, 0:n], AF.Identity, scale=modv[l][:, gi, c, s:s + 1], bias=modv[l][:, si, c, s:s + 1],
                          R=[t.d, modv[l].d], W=[hT.ds[c]])
            if l == 0: dbg("hT%d" % gi, hT[:, :, 0:S.W], R=hT.ds); dbg("rstd%d" % gi, rstd[:, 0:S.W], R=[rstd.d])

        def post_norm(S, l, m, gai, ph):
            rms_stats(S, None, ph)
            tmp = ph.tmp
            jobs = [(c0, n, s, c) for (c0, n, s) in S.segs for c in range(8)]

            def first(i):
                c0, n, s, c = jobs[i]; t = tmp[i % 3]
                K.tt("dve", t[:, 0:n], m[:, c, c0:c0 + n], rstd[:, c0:c0 + n], ALU.mult, R=[m.ds[c], rstd.d], W=[t.d])

            def second(i):
                c0, n, s, c = jobs[i]; t = tmp[i % 3]
                K.stt(x[:, c, c0:c0 + n], t[:, 0:n], modv[l][:, gai, c, s:s + 1], x[:, c, c0:c0 + n], ALU.mult, ALU.add,
                      R=[t.d, modv[l].d], W=[x.ds[c]])
            first(0)
            for i in range(len(jobs)):
                if i + 1 < len(jobs): first(i + 1)
                second(i)

        def proj(S, slab, src, nk=8):
            b = bank()
            for kc in range(nk):
                K.mm(b.t[:, 0:S.W], slab[:, kc, :], src[:, kc, 0:S.W], start=(kc == 0), stop=(kc == nk - 1), R=[slab.d, src.ds[kc]], W=[b.d])
            return b

        def seg3(S, ap2d):
            return ap2d.rearrange("p (s t) -> p s t", s=S.ns)

        def load_x(S, p):
            for (c0, n, s) in S.segs:
                for blk0 in range(0, n, 128):
                    nb = min(128, n - blk0)
                    src = xs[blk0:blk0 + nb, :] if s == 2 else xp[s, p * TT + blk0:p * TT + blk0 + nb, :]
                    K.dma("sp", q_stage, stage[0:nb, :], src, W=[stage.d])
                    for half in range(2):
                        b = bank()
                        for c4 in range(4):
                            c = half * 4 + c4
                            K.tr(b.t[:, c4 * 128:c4 * 128 + nb], stage[0:nb, c * 128:(c + 1) * 128], identf[0:nb, 0:nb], R=[stage.d, cst], W=[b.d])
                        K.cp("act" if half else "dve", x[:, half * 4:half * 4 + 4, c0 + blk0:c0 + blk0 + nb],
                             b.t[:, :].rearrange("p (c t) -> p c t", t=128)[:, :, 0:nb], W=[b.d] + x.ds[half * 4:half * 4 + 4]); rel(b)

        def store_x(S, p):
            for (c0, n, s) in S.segs:
                for blk0 in range(0, n, 128):
                    nb = min(128, n - blk0)
                    for half in range(2):
                        b = bank()
                        for c4 in range(4):
                            c = half * 4 + c4
                            K.tr(b.t[0:nb, c4 * 128:(c4 + 1) * 128], x[:, c, c0 + blk0:c0 + blk0 + nb], identf[:, :], R=[x.ds[c], cst], W=[b.d])
                        K.cp("act" if half else "dve", stage[0:nb, half * 512:(half + 1) * 512], b.t[0:nb, :], W=[b.d, stage.d]); rel(b)
                    dst = ys[blk0:blk0 + nb, :] if s == 2 else yp[s, p * TT + blk0:p * TT + blk0 + nb, :]
                    K.dma("sp", q_stage, dst, stage[0:nb, :], R=[stage.d])

        def conv_phase(S, l, last):
            W = S.W; V = vec[l]; full = S.full[l]; n = S.tt
            vc = lambda nm, a=0, k=1: V[:, VOFF[nm] + a:VOFF[nm] + a + k]
            with contextlib.ExitStack() as ph:
                K.barrier()
                norm_mod(S, l, 1, 0, ph)
                tgs = [sb([128, 512], es=ph) for _ in range(2)]; ycv = sb([128, 4, 512], es=ph, nd=4); ysq = sb([128, 4, 512], BF16, es=ph, nd=4)
                ybf = sb([128, 4, 512], BF16, es=ph, nd=4)
                diags = [sb([128, CW, 128], BF16, es=ph) for _ in range(2)]; mean = sb([128, 512], es=ph); var = sb([128, 512], es=ph)
                t1 = sb([128, 512], es=ph); t2 = sb([128, 512], es=ph)

                def glu_c(c):
                    tg = tgs[c % 2]; diag = diags[c % 2]
                    K.tt("dve", diag[:, :, :], bc(ident[:, :], 1, CW), vc("conv_dw", c * CW, CW).unsqueeze(2).broadcast_to([128, CW, 128]), ALU.mult,
                         R=[cst, V.d], W=[diag.d])
                    bg = proj(S, next_slab(("w_in", l, 4 + c)), hT)
                    K.act(tg[:, 0:W], bg.t[:, 0:W], AF.Tanh, scale=0.5, W=[bg.d, tg.d]); rel(bg)
                    K.act(tg[:, 0:W], tg[:, 0:W], AF.Identity, scale=0.5, bias=0.5, W=[tg.d])
                    ba = proj(S, next_slab(("w_in", l, c)), hT)
                    if last:
                        K.tt("dve", S.gl[:, c, :, :], seg3(S, ba.t[:, 0:W])[:, :, n - CB:n], seg3(S, tg[:, 0:W])[:, :, n - CB:n], ALU.mult,
                             R=[tg.d], W=[ba.d, S.gl.d])
                    K.tt("dve", full[:, c, :, CB:CB + n], seg3(S, ba.t[:, 0:W]), seg3(S, tg[:, 0:W]), ALU.mult, R=[tg.d], W=[ba.d, full.ds[c]]); rel(ba)

                def conv_c(c):
                    diag = diags[c % 2]
                    by = bank()
                    for j in range(CW):
                        K.mm(by.t[:, 0:W], diag[:, j, :], full[:, c, :, j:j + n], start=(j == 0), stop=(j == CW - 1), R=[diag.d, full.ds[c]], W=[by.d])
                    K.act(ycv[:, c, 0:W], by.t[:, 0:W], AF.Identity, bias=vc("conv_b", c), R=[V.d], W=[by.d, ycv.ds[c]])
                    K.act(ysq[:, c, 0:W], by.t[:, 0:W], AF.Square, bias=vc("conv_b", c), R=[V.d], W=[by.d, ysq.ds[c]])
                    K.act(ybf[:, c, 0:W], by.t[:, 0:W], AF.Identity, bias=vc("conv_b", c), R=[V.d], W=[by.d, ybf.ds[c]]); rel(by)
                    if l == 0 and c == 0: dbg("full0", full[:, 0, 0, :], R=full.ds); dbg("ycv0", ycv[:, 0, 0:W], R=ycv.ds)
                    K.cp("pool", full[:, c, :, 0:CB], full[:, c, :, n:n + CB], W=[full.ds[c]])
                glu_c(0)
                for c in range(4):
                    if c < 3: glu_c(c + 1)
                    conv_c(c)
                b1 = bank()
                for c in range(4):
                    K.mm(b1.t[:, 0:W], ones[:, :], ybf[:, c, 0:W], start=(c == 0), stop=(c == 3), R=[ybf.ds[c], cst], W=[b1.d])
                K.ts("dve", mean[:, 0:W], b1.t[:, 0:W], 1.0 / DC, None, ALU.mult, W=[b1.d, mean.d]); rel(b1)
                b2 = bank()
                for c in range(4):
                    K.mm(b2.t[:, 0:W], ones[:, :], ysq[:, c, 0:W], start=(c == 0), stop=(c == 3), R=[ysq.ds[c], cst], W=[b2.d])
                K.tt("dve", var[:, 0:W], mean[:, 0:W], mean[:, 0:W], ALU.mult, R=[mean.d], W=[var.d])
                K.stt(var[:, 0:W], b2.t[:, 0:W], 1.0 / DC, var[:, 0:W], ALU.mult, ALU.subtract, W=[b2.d, var.d]); rel(b2)
                K.act(var[:, 0:W], var[:, 0:W], AF.Ln, bias=LN_EPS, W=[var.d])
                K.act(var[:, 0:W], var[:, 0:W], AF.Exp, scale=-0.5, W=[var.d])
                t1s = [t1] + [sb([128, 512], es=ph) for _ in range(3)]; t2s = [t2] + [sb([128, 512], es=ph) for _ in range(3)]
                for c in range(4):
                    K.tt("dve", t1s[c][:, 0:W], ycv[:, c, 0:W], mean[:, 0:W], ALU.subtract, R=[ycv.ds[c], mean.d], W=[t1s[c].d])
                for c in range(4):
                    K.tt("dve", t1s[c][:, 0:W], t1s[c][:, 0:W], var[:, 0:W], ALU.mult, R=[var.d], W=[t1s[c].d])
                for c in range(4):
                    K.act(t2s[c][:, 0:W], t1s[c][:, 0:W], AF.Tanh, scale=vc("hlg", c), bias=vc("hlb", c), R=[t1s[c].d, V.d], W=[t2s[c].d])
                for c in range(4):
                    K.ts("dve", t1s[c][:, 0:W], t1s[c][:, 0:W], vc("hlg", c), vc("hlb", c), ALU.mult, ALU.add, R=[V.d], W=[t1s[c].d])
                for c in range(4):
                    K.stt(mixT[:, c, 0:W], t2s[c][:, 0:W], 1.0, t1s[c][:, 0:W], ALU.add, ALU.mult, R=[t1s[c].d, t2s[c].d], W=[mixT.ds[c]])
                if l == 0: dbg("mean", mean[:, 0:W], R=[mean.d]); dbg("crstd", var[:, 0:W], R=[var.d]); dbg("mixc", mixT[:, 0:4, 0:W], R=mixT.ds)

        def rwkv_phase(S, l, last):
            W = S.W; V = vec[l]; n = S.tt; ns = S.ns; zh = S.zh[l]
            SUB = min(128, W); NSUB = W // SUB
            CH = SUB; NCH = W // CH; CPS = 1; NLEV = 6 if CH == 128 else 5
            vc = lambda nm, a=0, k=1: V[:, VOFF[nm] + a:VOFF[nm] + a + k]
            with contextlib.ExitStack() as ph:
                K.barrier()
                def pad(src, dst, c):
                    K.cp("pool", dst[0:64, c, 0, 0:W], src[0:64, c, 0:W], R=[src.ds[c]], W=[dst.ds[c]])
                    K.cp("pool", dst[64:128, c, 1, 0:W], src[64:128, c, 0:W], R=[src.ds[c]], W=[dst.ds[c]])
                g4 = sb([128, 4, 512], BF16, es=ph, nd=4); bv = sb([128, 4, 512], BF16, es=ph, nd=4)
                ah, bt, kt, rh, vb = [sb([128, 4, 512], BF16, es=ph, nd=4) for _ in range(5)]
                bhp, khp = [sb([128, 4, 2, 512], BF16, es=ph, nd=8) for _ in range(2)]
                WC = sb([128, 4, 8], es=ph, nd=4)
                ph1 = contextlib.ExitStack()
                f = lambda dt=F32: sb([128, 512], dt, es=ph1)
                lwin = f(BF16); sg = f(BF16); zm0 = f()
                for pz in (bhp, khp):
                    K.op("pool", lambda e, pz=pz: e.memset(pz[:, :, :, :], 0.0), W=pz.ds)

                class TS: pass
                tsets = []
                for _ in range(2):
                    t = TS(); tsets.append(t)
                    t.zt = sb([128, 2, 257], es=ph1, nd=2)
                    t.zm, t.lw, t.Lc, t.aa, t.k0, t.kf, t.rs = [f() for _ in range(7)]
                    t.eL, t.enL, t.eLm, t.eCL, t.tb, t.kr, t.bh, t.kh = [f(BF16) for _ in range(8)]

                def pad(src, dst, c):
                    K.cp("pool", dst[0:64, c, 0, 0:W], src[0:64, 0:W], R=[src.d], W=[dst.ds[2 * c]])
                    K.cp("pool", dst[64:128, c, 1, 0:W], src[64:128, 0:W], R=[src.d], W=[dst.ds[2 * c + 1]])

                def mixz(j, zm, zt):
                    zi = j - 8
                    b = proj(S, next_slab(("w_in", l, j)), hT)
                    z3 = seg3(S, b.t[:, 0:W])
                    K.cp("act", zt[:, 0:ns, 0:1], zh[:, zi, :].unsqueeze(2), R=[zh.ds[zi]], W=[zt.ds[1]])
                    K.act(zt[:, 0:ns, 1:n + 1], z3, AF.Identity, scale=vc("mu", zi), R=[V.d], W=[b.d, zt.ds[0]])
                    K.act(zh[:, zi, :].unsqueeze(2), z3[:, :, n - 1:n], AF.Identity, scale=vc("mu", zi), R=[V.d], W=[b.d, zh.ds[zi]])
                    if last:
                        K.cp("act", S.zl[:, zi, :].unsqueeze(2), z3[:, :, n - 1:n], W=[b.d, S.zl.d])
                    K.stt(seg3(S, zm[:, 0:W]), z3, vc("omu", zi), zt[:, 0:ns, 0:n], ALU.mult, ALU.add, R=[V.d] + zt.ds, W=[b.d, zm.d]); rel(b)

                mixz(20, zm0, tsets[0].zt)
                if l == 0: dbg("zm20", zm0[:, 0:W], R=[zm0.d])
                K.act(lwin[0:64, 0:W], zm0[0:64, 0:W], AF.Tanh, R=[zm0.d], W=[lwin.d])
                K.cp("act", lwin[64:128, 0:W], zm0[64:128, 0:W], R=[zm0.d], W=[lwin.d])
                mixz(21, zm0, tsets[1].zt)
                K.act(sg[:, 0:W], zm0[:, 0:W], AF.Tanh, scale=0.5, R=[zm0.d], W=[sg.d])
                K.act(sg[:, 0:W], sg[:, 0:W], AF.Identity, scale=0.5, bias=0.5, W=[sg.d])

                def prep_ops(c, t):
                    o = []; add = o.append
                    cs = slice(c * 128, (c + 1) * 128)
                    zm, lw, Lc, aa, k0, kf, rs = t.zm, t.lw, t.Lc, t.aa, t.k0, t.kf, t.rs
                    eL, enL, eLm, eCL, tb, kr = t.eL, t.enL, t.eLm, t.eCL, t.tb, t.kr
                    hold = {}

                    def lora(lhsT, rhs, deps, fin):
                        b = bank()
                        K.mm(b.t[:, 0:W], lhsT, rhs, R=deps, W=[b.d])
                        fin(b); rel(b)
                    add(lambda: lora(wa2b[l][0:64, cs], lwin[0:64, 0:W], [wa2b[l].d, lwin.d],
                                     lambda b: K.act(lw[:, 0:W], b.t[:, 0:W], AF.Tanh, scale=0.5, bias=vc("hw0", c), R=[V.d], W=[b.d, lw.d])))
                    add(lambda: K.ts("dve", lw[:, 0:W], lw[:, 0:W], -0.5 * EM05, -0.5 * EM05, ALU.mult, ALU.add, W=[lw.d]))
                    add(lambda: lora(wa2b[l][64:128, cs], lwin[64:128, 0:W], [wa2b[l].d, lwin.d],
                                     lambda b: K.act(aa[:, 0:W], b.t[:, 0:W], AF.Tanh, scale=0.5, bias=vc("ha0", c), R=[V.d], W=[b.d, aa.d])))
                    add(lambda: K.ts("dve", aa[:, 0:W], aa[:, 0:W], 0.5, 0.5, ALU.mult, ALU.add, W=[aa.d]))
                    add(lambda: lora(g2b[l][:, cs], sg[:, 0:W], [g2b[l].d, sg.d],
                                     lambda b: K.cp("act", g4[:, c, 0:W], b.t[:, 0:W], W=[b.d, g4.ds[c]])))
                    add(lambda: K.op("dve", lambda e: e.tensor_tensor_scan(out=Lc[:, 0:W], data0=rmask[:, 0:W], data1=lw[:, 0:W], initial=0.0, op0=ALU.mult, op1=ALU.add),
                                     R=[lw.d, cst], W=[Lc.d]))
                    add(lambda: K.tt("dve", lw[:, 0:W], Lc[:, 0:W], lw[:, 0:W], ALU.subtract, R=[Lc.d], W=[lw.d]))
                    add(lambda: K.act(eL[:, 0:W], Lc[:, 0:W], AF.Exp, R=[Lc.d], W=[eL.d]))
                    add(lambda: K.act(enL[:, 0:W], Lc[:, 0:W], AF.Exp, scale=-1.0, R=[Lc.d], W=[enL.d]))
                    add(lambda: K.act(eLm[:, 0:W], lw[:, 0:W], AF.Exp, R=[lw.d], W=[eLm.d]))
                    L3 = Lc[:, 0:W].rearrange("p (c t) -> p c t", t=CH)
                    add(lambda: K.act(WC[:, c, 0:NCH].unsqueeze(2), L3[:, :, CH - 1:CH], AF.Exp, R=[Lc.d], W=[WC.ds[c]]))
                    add(lambda: K.tt("dve", lw[:, 0:W].rearrange("p (c t) -> p c t", t=CH), L3[:, :, CH - 1:CH].broadcast_to([128, NCH, CH]), L3, ALU.subtract,
                                     R=[Lc.d, eLm.d], W=[lw.d]))
                    add(lambda: K.act(eCL[:, 0:W], lw[:, 0:W], AF.Exp, R=[lw.d], W=[eCL.d]))
                    add(lambda: mixz(12 + c, zm, t.zt))
                    add(lambda: K.ts("dve", k0[:, 0:W], zm[:, 0:W], vc("k_k", c), None, ALU.mult, R=[zm.d, V.d], W=[k0.d]))
                    add(lambda: K.act(tb[:, 0:W], k0[:, 0:W], AF.Square, R=[k0.d], W=[tb.d]))
                    add(lambda: lora(blk[:, :], tb[:, 0:W], [tb.d, cst],
                                     lambda b: K.act(rs[:, 0:W], b.t[:, 0:W], AF.Ln, bias=L2_EPS, W=[b.d, rs.d])))
                    add(lambda: K.act(rs[:, 0:W], rs[:, 0:W], AF.Exp, scale=-0.5, W=[rs.d]))
                    add(lambda: K.tt("dve", k0[:, 0:W], k0[:, 0:W], rs[:, 0:W], ALU.mult, R=[rs.d], W=[k0.d]))
                    add(lambda: K.ts("dve", kf[:, 0:W], aa[:, 0:W], vc("k_a", c), vc("omka", c), ALU.mult, ALU.add, R=[aa.d, V.d], W=[kf.d]))
                    add(lambda: K.tt("dve", kf[:, 0:W], kf[:, 0:W], zm[:, 0:W], ALU.mult, R=[zm.d], W=[kf.d]))
                    add(lambda: K.stt(ah[:, c, 0:W], k0[:, 0:W], -1.0, eLm[:, 0:W], ALU.mult, ALU.mult, R=[k0.d, eLm.d], W=[ah.ds[c]]))
                    add(lambda: K.tt("dve", rs[:, 0:W], k0[:, 0:W], aa[:, 0:W], ALU.mult, R=[k0.d, aa.d], W=[rs.d]))
                    add(lambda: K.tt("dve", t.bh[:, 0:W], rs[:, 0:W], enL[:, 0:W], ALU.mult, R=[rs.d, enL.d], W=[t.bh.d]))
                    add(lambda: pad(t.bh, bhp, c))
                    add(lambda: K.tt("dve", t.kh[:, 0:W], kf[:, 0:W], enL[:, 0:W], ALU.mult, R=[kf.d, enL.d], W=[t.kh.d]))
                    add(lambda: pad(t.kh, khp, c))
                    add(lambda: K.tt("pool", bt[:, c, 0:W], rs[:, 0:W], eCL[:, 0:W], ALU.mult, R=[rs.d, eCL.d], W=[bt.ds[c]]))
                    add(lambda: K.tt("pool", kt[:, c, 0:W], kf[:, 0:W], eCL[:, 0:W], ALU.mult, R=[kf.d, eCL.d], W=[kt.ds[c]]))
                    add(lambda: K.act(kr[:, 0:W], kf[:, 0:W], AF.Identity, scale=vc("r_k", c), R=[kf.d, V.d], W=[kr.d]))
                    add(lambda: mixz(8 + c, zm, t.zt))
                    add(lambda: K.tt("dve", rh[:, c, 0:W], zm[:, 0:W], eL[:, 0:W], ALU.mult, R=[zm.d, eL.d], W=[rh.ds[c]]))
                    add(lambda: K.tt("dve", tb[:, 0:W], zm[:, 0:W], kr[:, 0:W], ALU.mult, R=[zm.d, kr.d], W=[tb.d]))

                    def bsum():
                        hold["bb"] = bb = bank()
                        K.mm(bb.t[:, 0:W], blk[:, :], tb[:, 0:W], R=[tb.d, cst], W=[bb.d])
                    add(bsum)
                    add(lambda: mixz(16 + c, zm, t.zt))
                    add(lambda: K.cp("act", vb[:, c, 0:W], zm[:, 0:W], R=[zm.d], W=[vb.ds[c]]))

                    def bvf():
                        bb = hold["bb"]
                        K.tt("dve", bv[:, c, 0:W], bb.t[:, 0:W], zm[:, 0:W], ALU.mult, R=[zm.d], W=[bb.d, bv.ds[c]]); rel(bb)
                    add(bvf)
                    return o

                oa = prep_ops(0, tsets[0]) + prep_ops(2, tsets[0]); ob = prep_ops(1, tsets[1]) + prep_ops(3, tsets[1])
                for i in range(len(oa) + PREP_OFFSET):
                    if i < len(oa): oa[i]()
                    if 0 <= i - PREP_OFFSET < len(ob): ob[i - PREP_OFFSET]()

                ph1.close()
                K.barrier()
                ST = S.ST[l]; STb = S.STb[l]

                def core_bufs():
                    A = lambda: sb([128, 8, 128], BF16, es=ph, nd=2)
                    mats = [A() for _ in range(9)]
                    vts = [sb([128, 512], BF16, es=ph) for _ in range(3)]
                    Xs = sb([128, 512], BF16, es=ph); Us = sb([128, 512], BF16, es=ph)
                    ysq = sb([128, 512], es=ph); yn = sb([128, 512], BF16, es=ph)
                    st8 = [sb([128, 8], es=ph) for _ in range(4)]; o1 = sb([128, 128], es=ph)
                    return (mats, vts, Xs, Us, ysq, yn, st8, o1)
                bufsets = [core_bufs() for _ in range(2 if NSUB > 1 else 1)]

                def core_body(q, B, yb):
                    (AakT, ArbT, ArkT, Pm, Ak, AkT, Ak2, AkT2, Pt), (vT, btT, ktT), Xs, Us, ysq, yn, st8, o1 = B
                    q0 = q * SUB; qs = slice(q0, q0 + SUB); seg = q0 // n
                    for src, dst in [(vb, vT), (bt, btT), (kt, ktT)]:
                        b = bank(); pb = b.t[:, :].bitcast(BF16)
                        for c in range(4):
                            K.tr(pb[0:SUB, c * 128:(c + 1) * 128], src[:, c, qs], ident[:, :], R=[src.ds[c], cst], W=[b.d])
                        K.cp("act", dst[0:SUB, :], pb[0:SUB, 0:512], W=[b.d, dst.d]); rel(b)

                    def amat(lh, lpad, rh_, rpad, mask, dst):
                        for hb in range(2):
                            b = bank()
                            for h4 in range(4):
                                h = hb * 4 + h4; c = h // 2
                                la = lh[:, c, h % 2, qs] if lpad else lh[:, c, qs]
                                ra = rh_[:, c, h % 2, qs] if rpad else rh_[:, c, qs]
                                K.mm(b.t[0:SUB, h4 * 128:h4 * 128 + SUB], la, ra, R=[lh.ds[h] if lpad else lh.ds[c], rh_.ds[h] if rpad else rh_.ds[c]], W=[b.d])
                            K.tt("dve", dst[0:SUB, hb * 4:hb * 4 + 4, 0:SUB], b.t[0:SUB, :].rearrange("p (h t) -> p h t", t=128)[:, :, 0:SUB],
                                 bc(mask[0:SUB, 0:SUB], 1, 4), ALU.mult, R=[cst], W=[b.d, dst.ds[hb]]); rel(b)
                    amat(bhp, 1, ah, 0, mts, AkT)
                    amat(ah, 0, bhp, 1, ms, Ak)
                    amat(khp, 1, ah, 0, mts, AakT)
                    amat(bhp, 1, rh, 0, mti, ArbT)
                    amat(khp, 1, rh, 0, mti, ArkT)
                    K.tt("pool", Pm[0:SUB, :, 0:SUB], AkT[0:SUB, :, 0:SUB], bc(ident[0:SUB, 0:SUB], 1, 8), ALU.add, R=AkT.ds + [cst], W=Pm.ds)
                    ca, cat, na, nat = Ak, AkT, Ak2, AkT2
                    for lev in range(NLEV):
                        for hb in range(2):
                            b = bank()
                            for h4 in range(4):
                                h = hb * 4 + h4
                                K.mm(b.t[0:SUB, h4 * 128:h4 * 128 + SUB], cat[0:SUB, h, 0:SUB], ca[0:SUB, h, 0:SUB], R=[cat.ds[hb], ca.ds[hb]], W=[b.d])
                            K.cp("act", na[0:SUB, hb * 4:hb * 4 + 4, 0:SUB], b.t[0:SUB, :].rearrange("p (h t) -> p h t", t=128)[:, :, 0:SUB], W=[b.d, na.ds[hb]]); rel(b)
                        if lev < NLEV - 1:
                            for hb in range(2):
                                b = bank()
                                for h4 in range(4):
                                    h = hb * 4 + h4
                                    K.mm(b.t[0:SUB, h4 * 128:h4 * 128 + SUB], ca[0:SUB, h, 0:SUB], cat[0:SUB, h, 0:SUB], R=[cat.ds[hb], ca.ds[hb]], W=[b.d])
                                K.cp("act", nat[0:SUB, hb * 4:hb * 4 + 4, 0:SUB], b.t[0:SUB, :].rearrange("p (h t) -> p h t", t=128)[:, :, 0:SUB], W=[b.d, nat.ds[hb]]); rel(b)
                        for hb in range(2):
                            b = bank()
                            for h4 in range(4):
                                h = hb * 4 + h4
                                K.mm(b.t[0:SUB, h4 * 128:h4 * 128 + SUB], na[0:SUB, h, 0:SUB], Pm[0:SUB, h, 0:SUB], R=[na.ds[hb], Pm.ds[hb]], W=[b.d])
                            K.tt("dve", Pt[0:SUB, hb * 4:hb * 4 + 4, 0:SUB], b.t[0:SUB, :].rearrange("p (h t) -> p h t", t=128)[:, :, 0:SUB],
                                 Pm[0:SUB, hb * 4:hb * 4 + 4, 0:SUB], ALU.add, R=[Pm.ds[hb]], W=[b.d, Pt.ds[hb]]); rel(b)
                        Pm, Pt = Pt, Pm
                        ca, cat, na, nat = na, nat, ca, cat
                    if l == 0 and q == 0:
                        dbg("vT", vT[0:SUB, :], R=[vT.d]); dbg("AabT", ca[0:SUB, 0, 0:SUB] if False else AakT[0:SUB, 0, 0:SUB], R=[AakT.d])
                        dbg("TT", Pm[0:SUB, 0, 0:SUB], R=[Pm.d]); dbg("ArkT", ArkT[0:SUB, 0, 0:SUB], R=[ArkT.d])
                    for ci in range(CPS):
                        r0 = 0; rsl = slice(0, CH); cols = slice(q0, q0 + CH)
                        gch = q0 // CH
                        bX = bank()
                        for c in range(4):
                            K.mm(bX.t[rsl, c * 128:(c + 1) * 128], ah[:, c, cols], STb[:, c, seg, :, :], start=True, stop=False, R=[ah.ds[c], STb.d], W=[bX.d])
                            for h in (2 * c, 2 * c + 1):
                                hs = slice(h * 64, (h + 1) * 64)
                                K.mm(bX.t[rsl, hs], AakT[rsl, h, rsl], vT[rsl, hs], start=False, stop=(h == 2 * c + 1), R=[AakT.ds[h // 4], vT.d], W=[bX.d])
                        K.cp("act", Xs[rsl, :], bX.t[rsl, :], W=[bX.d, Xs.d]); rel(bX)
                        bU = bank()
                        for h in range(8):
                            hs = slice(h * 64, (h + 1) * 64)
                            K.mm(bU.t[rsl, hs], Pm[rsl, h, rsl], Xs[rsl, hs], R=[Pm.ds[h // 4], Xs.d], W=[bU.d])
                        K.cp("dve", Us[rsl, :], bU.t[rsl, :], W=[bU.d, Us.d]); rel(bU)
                        for c in range(4):
                            K.mm(yb.t[rsl, c * 128:(c + 1) * 128], rh[:, c, cols], STb[:, c, seg, :, :], start=True, stop=False, R=[rh.ds[c], STb.d], W=[yb.d])
                            for h in (2 * c, 2 * c + 1):
                                hs = slice(h * 64, (h + 1) * 64)
                                K.mm(yb.t[rsl, hs], ArbT[rsl, h, rsl], Us[rsl, hs], start=False, stop=False, R=[ArbT.ds[h // 4], Us.d], W=[yb.d])
                                K.mm(yb.t[rsl, hs], ArkT[rsl, h, rsl], vT[rsl, hs], start=False, stop=(h == 2 * c + 1), R=[ArkT.ds[h // 4], vT.d], W=[yb.d])
                        bS = bank()
                        for h in range(8):
                            c = h // 2; p0 = 64 * (h % 2); hs = slice(h * 64, (h + 1) * 64)
                            K.mm(bS.t[p0:p0 + 64, c * 64:(c + 1) * 64], btT[rsl, hs], Us[rsl, hs], start=True, stop=False, R=[btT.d, Us.d], W=[bS.d])
                            K.mm(bS.t[p0:p0 + 64, c * 64:(c + 1) * 64], ktT[rsl, hs], vT[rsl, hs], start=False, stop=True, R=[ktT.d, vT.d], W=[bS.d])
                        for c in range(4):
                            K.stt(ST[:, c, seg, :], ST[:, c, seg, :], WC[:, c, gch:gch + 1], bS.t[:, c * 64:(c + 1) * 64], ALU.mult, ALU.add,
                                  R=[WC.ds[c]], W=[bS.d, ST.ds[c]])
                        rel(bS)
                        K.cp("act", STb[0:64, :, seg, 0, :], ST[0:64, :, seg, :], R=ST.ds, W=[STb.d])
                        K.cp("act", STb[64:128, :, seg, 1, :], ST[64:128, :, seg, :], R=ST.ds, W=[STb.d])
                    if l == 0 and q == 0:
                        dbg("Xs", Xs[0:SUB, :], R=[Xs.d]); dbg("Us", Us[0:SUB, :], R=[Us.d]); dbg("Y", yb.t[0:SUB, :], R=[yb.d])
                        dbg("ST", ST[:, :, 0, :], R=ST.ds)
                    Y3 = yb.t[0:SUB, :].rearrange("p (h v) -> p h v", v=64)
                    sm, sq_, mu_, rs_ = st8
                    K.op("dve", lambda e: e.tensor_reduce(out=sm[0:SUB, :], in_=Y3, op=ALU.add, axis=AX.X), W=[yb.d, sm.d])
                    K.act(ysq[0:SUB, :], yb.t[0:SUB, :], AF.Square, W=[yb.d, ysq.d])
                    K.op("dve", lambda e: e.tensor_reduce(out=sq_[0:SUB, :], in_=ysq[0:SUB, :].rearrange("p (h v) -> p h v", v=64), op=ALU.add, axis=AX.X),
                         R=[ysq.d], W=[sq_.d])
                    K.ts("dve", mu_[0:SUB, :], sm[0:SUB, :], 1.0 / HD, None, ALU.mult, R=[sm.d], W=[mu_.d])
                    K.tt("dve", sm[0:SUB, :], mu_[0:SUB, :], mu_[0:SUB, :], ALU.mult, R=[mu_.d], W=[sm.d])
                    K.stt(rs_[0:SUB, :], sq_[0:SUB, :], 1.0 / HD, sm[0:SUB, :], ALU.mult, ALU.subtract, R=[sq_.d, sm.d], W=[rs_.d])
                    K.act(rs_[0:SUB, :], rs_[0:SUB, :], AF.Ln, bias=GN_EPS, W=[rs_.d])
                    K.act(rs_[0:SUB, :], rs_[0:SUB, :], AF.Exp, scale=-0.5, W=[rs_.d])
                    K.tt("dve", ysq[0:SUB, :].rearrange("p (h v) -> p h v", v=64), Y3, mu_[0:SUB, :].unsqueeze(2).broadcast_to([SUB, 8, 64]), ALU.subtract,
                         R=[mu_.d], W=[yb.d, ysq.d])
                    K.tt("dve", yn[0:SUB, :].rearrange("p (h v) -> p h v", v=64), ysq[0:SUB, :].rearrange("p (h v) -> p h v", v=64),
                         rs_[0:SUB, :].unsqueeze(2).broadcast_to([SUB, 8, 64]), ALU.mult, R=[rs_.d, ysq.d], W=[yn.d])
                    b = bank(); pb = b.t[:, :].bitcast(BF16)
                    for c in range(4):
                        K.tr(pb[:, c * 128:c * 128 + SUB], yn[0:SUB, c * 128:(c + 1) * 128], ident[0:SUB, 0:SUB], R=[yn.d, cst], W=[b.d])
                    for c in range(4):
                        K.act(o1[:, 0:SUB], pb[:, c * 128:c * 128 + SUB], AF.Identity, scale=vc("lnx_g", c), bias=vc("lnx_b", c), R=[V.d], W=[b.d, o1.d])
                        K.tt("dve", o1[:, 0:SUB], o1[:, 0:SUB], bv[:, c, qs], ALU.add, R=[bv.ds[c]], W=[o1.d])
                        K.tt("dve", mixT[:, 4 + c, qs], o1[:, 0:SUB], g4[:, c, qs], ALU.mult, R=[o1.d, g4.ds[c]], W=[mixT.ds[4 + c]])
                    rel(b)

                if NSUB == 1:
                    core_body(0, bufsets[0], banks[5])
                else:
                    for q in range(NSUB // 2):
                        recs = []
                        for si, rn in ((0, "A"), (1, "B")):
                            cur_ring[0] = rn
                            recs.append(K.record(lambda: core_body(si * (NSUB // 2) + q, bufsets[si], banks[5 + si])))
                        cur_ring[0] = "all"
                        K.replay(*recs)
                if l == 0: dbg("mixr", mixT[:, 4:8, 0:W], R=mixT.ds)

        def outproj_phase(S, l, which):
            pass

        def layer_pass(S, l, last):
            W = S.W
            conv_phase(S, l, last)
            rwkv_phase(S, l, last)
            with contextlib.ExitStack() as ph:
                K.barrier()
                m = sb([128, 8, 512], es=ph, nd=8)
                for j in range(8):
                    b = proj(S, next_slab(("w_out", l, j)), mixT)
                    K.cp("act", m[:, j, 0:W], b.t[:, 0:W], W=[b.d, m.ds[j]])
                    K.act(phase_bufs(ph)[:, j, 0:W], b.t[:, 0:W], AF.Square, W=[b.d, ph.sq.ds[j]]); rel(b)
                if l == 0: dbg("m1", m[:, :, 0:W], R=m.ds)
                post_norm(S, l, m, 2, ph)
                if l == 0: dbg("x1", x[:, :, 0:W], R=x.ds)
                norm_mod(S, l, 4, 3, ph)
                act = sb([128, 22, 512], BF16, es=ph, nd=22)
                tg = [sb([128, 512], es=ph) for _ in range(2)]; tu = [sb([128, 512], es=ph) for _ in range(2)]
                for j in range(22):
                    t = tg[j % 2]; u = tu[j % 2]
                    bg = proj(S, next_slab(("w_gate", l, j)), hT)
                    K.act(t[:, 0:W], bg.t[:, 0:W], AF.Tanh, scale=0.5, W=[bg.d, t.d])
                    K.stt(u[:, 0:W], t[:, 0:W], 1.0, bg.t[:, 0:W], ALU.add, ALU.mult, R=[t.d], W=[bg.d, u.d]); rel(bg)
                    bu = proj(S, next_slab(("w_up", l, j)), hT)
                    K.stt(act[:, j, 0:W], u[:, 0:W], 0.5, bu.t[:, 0:W], ALU.mult, ALU.mult, R=[u.d], W=[bu.d, act.ds[j]]); rel(bu)
                for j in range(8):
                    b = proj(S, next_dslab(("w_down", l, j)), act, nk=22)
                    K.cp("act", m[:, j, 0:W], b.t[:, 0:W], W=[b.d, m.ds[j]])
                    K.act(phase_bufs(ph)[:, j, 0:W], b.t[:, 0:W], AF.Square, W=[b.d, ph.sq.ds[j]]); rel(b)
                if l == 0: dbg("act", act[:, 0:4, 0:W], R=act.ds); dbg("m2", m[:, :, 0:W], R=m.ds)
                post_norm(S, l, m, 5, ph)
                if l == 0: dbg("x2", x[:, :, 0:W], R=x.ds)

        def store_states(S, l):
            for si, (c0, n, s) in enumerate(S.segs):
                b = bank()
                for c in range(4):
                    K.tr(b.t[0:CB, c * 128:(c + 1) * 128], S.gl[:, c, si, :], identf[:, :], R=[S.gl.d, cst], W=[b.d])
                K.cp("dve", stage[0:CB, 0:512], b.t[0:CB, :], W=[b.d, stage.d]); rel(b)
                K.dma("sp", q_stage, o_conv[l, s], stage[0:CB, 0:512], R=[stage.d])
                b = bank()
                K.tr(b.t[0:14, 0:128], S.zl[:, :, si], identf[:, :], R=[S.zl.d, cst], W=[b.d])
                K.cp("dve", stage[0:14, 0:128], b.t[0:14, 0:128], W=[b.d, stage.d]); rel(b)
                K.dma("sp", q_stage, o_shift[l, s], stage[0:14, 0:128], R=[stage.d])
                b = bank()
                for c in range(4):
                    K.tr(b.t[0:64, c * 128:(c + 1) * 128], S.ST[l][:, c, si, :], identf[:, :], R=[S.ST[l].ds[c], cst], W=[b.d])
                K.cp("dve", stage[0:64, 0:512], b.t[0:64, :], W=[b.d, stage.d]); rel(b)
                K.dma("sp", q_stage, o_wkv[l, s].rearrange("h v k -> v h k"), stage[0:64, 0:512].rearrange("v (h k) -> v h k", k=64), R=[stage.d])

        for p in passes:
            S = SS if p == "s" else PS
            dbg_on[0] = (p == "s")
            last = (p == "s") or (p == NPASS - 1)
            load_x(S, 0 if p == "s" else p)
            for l in range(L):
                layer_pass(S, l, last)
                if last: store_states(S, l)
            store_x(S, 0 if p == "s" else p)
        K.barrier()
    if learn:
        return [j for (n, l, j) in sched if n == "w_in"][0:22]
    return nc


PREP_OFFSET = 14
WIN_ORDER = []


def _slabify(w):
    Lw, Kd, Nd = w.shape
    return np.ascontiguousarray(w.reshape(Lw, Kd // 128, 128, Nd // 128, 128).transpose(0, 3, 2, 1, 4).reshape(Lw, Nd // 128, 128, Kd))


def _colmajor(v):
    Lw, n = v.shape
    return v.reshape(Lw, n // 128, 128).transpose(0, 2, 1)


_NC_CACHE = {}


def kernel(x_prompt, x_sample, cache_conv, state_shift, state_wkv, c_prompt, c_sample, w_mod, b_mod, g_mix_pre, g_mix_post,
           g_ffn_pre, g_ffn_post, w_in, conv_dw, conv_b, conv_ln_g, conv_ln_b, mu_shift, w0, w2, a0, a2, g2, k_k, k_a, r_k,
           lnx_g, lnx_b, w_out, w_gate, w_up, w_down):
    f = lambda a: np.ascontiguousarray(np.asarray(a, dtype=np.float32))
    x_prompt = f(x_prompt); SEQ = x_prompt.shape[1]; B = x_prompt.shape[0]
    nco = B // 2
    if SEQ not in _NC_CACHE:
        if not WIN_ORDER:
            WIN_ORDER.extend(build(TT, learn=True))
            assert sorted(WIN_ORDER) == list(range(22)), WIN_ORDER
        _NC_CACHE[SEQ] = build(SEQ)
    nc = _NC_CACHE[SEQ]
    parts = [_colmajor(f(v)) for v in (g_mix_pre, g_mix_post, g_ffn_pre, g_ffn_post, b_mod, conv_b, conv_ln_g, conv_ln_b, mu_shift,
                                       w0, a0, k_k, k_a, f(r_k).reshape(L, DR), lnx_g, lnx_b)]
    cdw = f(conv_dw).reshape(L, CW, 4, 128).transpose(0, 3, 2, 1).reshape(L, 128, 4 * CW)
    vecs = np.ascontiguousarray(np.concatenate(parts + [cdw], axis=2))
    assert vecs.shape[2] == NV_IN
    shared = {
        "wmod": f(w_mod), "vecs": vecs, "wa2": np.ascontiguousarray(np.concatenate([f(w2), f(a2)], axis=1)), "g2": f(g2),
        "w_in": _slabify(f(w_in)), "w_out": _slabify(f(w_out)), "w_gate": _slabify(f(w_gate)), "w_up": _slabify(f(w_up)),
        "w_down": _slabify(f(w_down)),
    }
    x_sample = f(x_sample); cache_conv = f(cache_conv); state_shift = f(state_shift); state_wkv = f(state_wkv)
    c_prompt = f(c_prompt); c_sample = f(c_sample)
    in_maps = []
    for i in range(nco):
        call = np.stack([c_prompt[2 * i], c_prompt[2 * i + 1], c_sample[i]], axis=0)
        m = dict(shared)
        m.update({
            "xp": np.ascontiguousarray(x_prompt[2 * i:2 * i + 2]), "xs": np.ascontiguousarray(x_sample[i]),
            "cT": np.ascontiguousarray(call.reshape(3, 8, 128).transpose(2, 1, 0)),
            "cconv": np.ascontiguousarray(cache_conv[:, i]), "sshift": np.ascontiguousarray(state_shift[:, i].reshape(L, 14, 128)),
            "swkv": np.ascontiguousarray(state_wkv[:, i]),
        })
        in_maps.append(m)
    res = run_bass_kernel_spmd(nc, in_maps, core_ids=list(range(nco))).results
    if DEBUG:
        global LAST_RES
        LAST_RES = res
    yp = np.concatenate([r["yp"] for r in res], axis=0)
    ys = np.stack([r["ys"] for r in res], axis=0)
    oc = np.stack([r["o_conv"] for r in res], axis=0)
    osh = np.stack([r["o_shift"] for r in res], axis=0).reshape(nco, L, 3, DSH)
    ow = np.stack([r["o_wkv"] for r in res], axis=0)
    pr = lambda a: np.ascontiguousarray(np.moveaxis(a[:, :, 0:2], 1, 0).reshape((L, 2 * nco) + a.shape[3:]))
    sm = lambda a: np.ascontiguousarray(np.moveaxis(a[:, :, 2], 1, 0))
    return (yp, ys, pr(oc), pr(osh), pr(ow), sm(oc), sm(osh), sm(ow))
```

```python
import contextlib
import numpy as np
import concourse.bass as bass
import concourse.mybir as mybir
from concourse.bass_utils import run_bass_kernel_spmd

F32 = mybir.dt.float32
BF16 = mybir.dt.bfloat16
ALU = mybir.AluOpType
AF = mybir.ActivationFunctionType
AX = mybir.AxisListType

D = 1024; DC = 512; DR = 512; NH = 8; HD = 64; CW = 31; CB = 30
DSH = 1792; DIN = 2816; DFF = 2816; L = 2
RMS_EPS = 1e-6; LN_EPS = 1e-5; GN_EPS = 64e-5; L2_EPS = 1e-12
EM05 = float(np.exp(-0.5))
TT = 256
DEBUG = False
NCORES = 8

VOFF = {}
_o = 0
for _n, _k in [("g_mix_pre", 8), ("g_mix_post", 8), ("g_ffn_pre", 8), ("g_ffn_post", 8), ("b_mod", 48),
               ("conv_b", 4), ("conv_ln_g", 4), ("conv_ln_b", 4), ("mu", 14), ("w0", 4), ("a0", 4), ("k_k", 4),
               ("k_a", 4), ("r_k", 4), ("lnx_g", 4), ("lnx_b", 4), ("conv_dw", 124),
               ("omu", 14), ("hw0", 4), ("ha0", 4), ("omka", 4), ("hlg", 4), ("hlb", 4)]:
    VOFF[_n] = _o; _o += _k
NV = _o
NV_IN = VOFF["omu"]


class Dep:
    __slots__ = ("w", "r")

    def __init__(self):
        self.w = None; self.r = {}


class T:
    def __init__(self, t, nd=1):
        self.t = t; self.ds = [Dep() for _ in range(nd)]

    def __getitem__(self, k):
        return self.t[k]

    @property
    def d(self):
        return self.ds[0]


def bc(ap, pos, n):
    dims = [list(x) for x in ap.ap]
    dims.insert(pos, [0, n])
    return bass.AP(ap.tensor, ap.offset, dims)


class Builder:
    def __init__(self, nc, es):
        self.nc = nc; self.es = es
        self.eng = {"pe": nc.tensor, "act": nc.scalar, "dve": nc.vector, "pool": nc.gpsimd, "sp": nc.sync}
        self.sem = {e: es.enter_context(nc.semaphore("s_" + e)) for e in self.eng}
        self.cnt = {e: 0 for e in self.eng}
        self.known = {e: {} for e in self.eng}
        self.dcnt = {}
        self.uid = 0
        self.rec = None
        self.bankdeps = set()

    def name(self, p):
        self.uid += 1
        return "%s%d" % (p, self.uid)

    def sb(self, shape, dt=F32, nd=1, es=None, name="t"):
        return T((es or self.es).enter_context(self.nc.sbuf_tensor(self.name(name), list(shape), dt)), nd)

    def dslot(self, name):
        k = self.name("dq_" + name)
        self.sem[k] = self.es.enter_context(self.nc.semaphore(k)); self.dcnt[k] = 0
        return k

    def _wait(self, e, key, val, raw=False):
        if key == e and (e == "pe" or not raw):
            return
        if self.known[e].get(key, 0) >= val:
            return
        self.eng[e].wait_ge(self.sem[key], val); self.known[e][key] = val

    def _split(self, e, R, W):
        if e == "pe" or not self.bankdeps: return R, W, ()
        X = [d for d in W if id(d) in self.bankdeps]
        if not X: return R, W, ()
        return R, [d for d in W if id(d) not in self.bankdeps], X

    def _deps(self, e, R, W, X=(), rmw=True):
        for d in R:
            if d.w: self._wait(e, *d.w, raw=True)
        for d in W:
            if d.w: self._wait(e, *d.w, raw=rmw)
            for k, v in d.r.items(): self._wait(e, k, v)
        for d in X:
            if d.w: self._wait(e, *d.w, raw=True)
            for k, v in d.r.items():
                if k != e: self._wait(e, k, v)

    def _mark(self, key, val, R, W, X=()):
        for d in R: d.r[key] = val
        for d in X: d.r[key] = val
        for d in W: d.w = (key, val); d.r = {}

    def record(self, body):
        self.rec = []
        body()
        r, self.rec = self.rec, None
        return r

    def replay(self, *streams):
        n = max(len(x) for x in streams)
        for i in range(n):
            for x in streams:
                if i < len(x): self.op(*x[i])

    def op(self, e, fn, R=(), W=(), rmw=True):
        if self.rec is not None:
            self.rec.append((e, fn, tuple(R), tuple(W), rmw)); return
        R, W, X = self._split(e, R, W)
        self._deps(e, R, W, X, rmw)
        ins = fn(self.eng[e])
        self.cnt[e] += 1
        ins.then_inc(self.sem[e], 1)
        self._mark(e, self.cnt[e], R, W, X)

    def dma(self, q, slot, out, in_, R=(), W=(), **kw):
        assert self.rec is None
        if q == "pool": slot = self.dslot("sw")
        if self.dcnt[slot]: self._wait(q, slot, self.dcnt[slot], raw=True)
        self._deps(q, R, W)
        ins = self.eng[q].dma_start(out=out, in_=in_, **kw)
        self.dcnt[slot] += 16
        ins.then_inc(self.sem[slot], 16)
        self._mark(slot, self.dcnt[slot], R, W)

    def barrier(self):
        for e in self.eng:
            for o in self.eng:
                if o != e and self.cnt[o]: self._wait(e, o, self.cnt[o])
            for k, v in self.dcnt.items():
                if v: self._wait(e, k, v)

    def mm(self, out, lhsT, rhs, start=True, stop=True, R=(), W=(), skip=False):
        self.op("pe", lambda e: e.matmul(out, lhsT=lhsT, rhs=rhs, start=start, stop=stop, skip_group_check=skip), R, W)

    def tr(self, out, in_, ident, R=(), W=()):
        self.op("pe", lambda e: e.transpose(out=out, in_=in_, identity=ident), R, W)

    @staticmethod
    def _same(out, *ins):
        n = out.tensor.name
        return any(hasattr(i, "tensor") and i.tensor.name == n for i in ins)

    def act(self, out, in_, func, scale=1.0, bias=0.0, R=(), W=()):
        self.op("act", lambda e: e.activation(out=out, in_=in_, func=func, scale=scale, bias=bias), R, W, self._same(out, in_, scale, bias))

    def tt(self, eng, out, in0, in1, op, R=(), W=()):
        self.op(eng, lambda e: e.tensor_tensor(out=out, in0=in0, in1=in1, op=op), R, W, self._same(out, in0, in1))

    def ts(self, eng, out, in0, s1, s2, op0, op1=None, R=(), W=()):
        rmw = self._same(out, in0, s1, s2)
        if op1 is None:
            self.op(eng, lambda e: e.tensor_scalar(out=out, in0=in0, scalar1=s1, scalar2=None, op0=op0), R, W, rmw)
        else:
            self.op(eng, lambda e: e.tensor_scalar(out=out, in0=in0, scalar1=s1, scalar2=s2, op0=op0, op1=op1), R, W, rmw)

    def stt(self, out, in0, scalar, in1, op0, op1, R=(), W=()):
        self.op("dve", lambda e: e.scalar_tensor_tensor(out=out, in0=in0, scalar=scalar, in1=in1, op0=op0, op1=op1), R, W, self._same(out, in0, scalar, in1))

    def cp(self, eng, out, in_, R=(), W=()):
        rmw = self._same(out, in_)
        if eng == "act":
            self.op("act", lambda e: e.copy(out=out, in_=in_), R, W, rmw)
        else:
            self.op(eng, lambda e: e.tensor_copy(out=out, in_=in_), R, W, rmw)


def build(SEQ):
    NPASS = SEQ // TT
    nc = bass.Bass("TRN2", target_bir_lowering=False)
    din = lambda n, s: nc.dram_tensor(n, list(s), F32, kind="ExternalInput").ap()
    dout = lambda n, s: nc.dram_tensor(n, list(s), F32, kind="ExternalOutput").ap()
    xp = din("xp", [2, SEQ, D]); xs = din("xs", [64, D]); cT = din("cT", [128, 8, 3])
    cconv = din("cconv", [L, CB, DC]); sshift = din("sshift", [L, 14, 128]); swkv = din("swkv", [L, NH, HD, HD])
    wmod = din("wmod", [L, D, 6 * D]); vecs = din("vecs", [L, 128, NV_IN])
    wa2 = din("wa2", [L, 128, DR]); g2 = din("g2", [L, 128, DR])
    WSPEC = [("w_in", 22, 1024), ("w_out", 8, 1024), ("w_gate", 22, 1024), ("w_up", 22, 1024), ("w_down", 8, 2816)]
    wsrc = {n: din(n, [L, ns, 128, wd]) for n, ns, wd in WSPEC}
    wscr = {n: nc.dram_tensor("scr_" + n, [L, ns, 128, wd], BF16).ap() for n, ns, wd in WSPEC}
    yp = dout("yp", [2, SEQ, D]); ys = dout("ys", [64, D])
    o_conv = dout("o_conv", [L, 3, CB, DC]); o_shift = dout("o_shift", [L, 3, 14, 128]); o_wkv = dout("o_wkv", [L, 3, NH, HD, HD])

    with contextlib.ExitStack() as es:
        K = Builder(nc, es)
        sb = K.sb
        banks = [T(es.enter_context(nc.psum_tensor("bank%d" % i, [128, 512], F32))) for i in range(8)]
        for b in banks: b.open = False
        K.bankdeps = {id(b.d) for b in banks}
        rings = {"all": [0, 1, 2, 3, 4, 7], "A": [0, 1, 2], "B": [3, 4, 7]}; rposd = {"all": 0, "A": 0, "B": 0}; cur_ring = ["all"]

        def bank():
            ring = rings[cur_ring[0]]
            for _ in range(len(ring)):
                b = banks[ring[rposd[cur_ring[0]] % len(ring)]]; rposd[cur_ring[0]] += 1
                if not b.open: break
            assert not b.open, "all psum ring banks are open"
            b.open = True
            return b

        def rel(b):
            b.open = False

        identf = sb([128, 128]); ident = sb([128, 128], BF16); ones = sb([128, 128], BF16); blk = sb([128, 128], BF16)
        onesf = sb([128, 128])
        mts = sb([128, 128], BF16); ms = sb([128, 128], BF16); mti = sb([128, 128], BF16); scr_f = sb([128, 128])
        rmask = sb([128, 512])
        cst = Dep()

        def pm(fn): K.op("pool", fn, W=[cst])
        pm(lambda e: e.memset(identf[:, :], 0.0))
        pm(lambda e: e.affine_select(out=identf[:, :], in_=identf[:, :], pattern=[[-1, 128]], compare_op=ALU.not_equal, fill=1.0, base=0, channel_multiplier=1))
        pm(lambda e: e.tensor_copy(out=ident[:, :], in_=identf[:, :]))
        pm(lambda e: e.memset(ones[:, :], 1.0))
        pm(lambda e: e.memset(onesf[:, :], 1.0))
        pm(lambda e: e.memset(blk[:, :], 0.0))
        pm(lambda e: e.memset(blk[0:64, 0:64], 1.0))
        pm(lambda e: e.memset(blk[64:128, 64:128], 1.0))
        for m, (pat, cm, cmp) in [(mts, ([[1, 128]], -1, ALU.is_gt)), (ms, ([[-1, 128]], 1, ALU.is_gt)), (mti, ([[1, 128]], -1, ALU.is_ge))]:
            pm(lambda e: e.memset(scr_f[:, :], 1.0))
            pm(lambda e, pat=pat, cm=cm, cmp=cmp: e.affine_select(out=scr_f[:, :], in_=scr_f[:, :], pattern=pat, compare_op=cmp, fill=0.0, base=0, channel_multiplier=cm))
            pm(lambda e, m=m: e.tensor_copy(out=m[:, :], in_=scr_f[:, :]))
        pm(lambda e: e.memset(rmask[:, :], 1.0))
        pm(lambda e: e.memset(rmask[:, :].rearrange("p (c t) -> p c t", t=128)[:, :, 0:1], 0.0))

        dbgt = sb([128, 512] if DEBUG else [128, 2]); dbgq = K.dslot("dbg"); dbg_on = [False]

        def dbg(name, ap, R=()):
            if not (DEBUG and dbg_on[0]): return
            shp = list(ap.shape); p = shp[0]; n = int(np.prod(shp[1:]))
            o = nc.dram_tensor("dbg_" + name, [p, n], F32, kind="ExternalOutput").ap()
            dv = dbgt[0:p, 0:n]
            if len(shp) == 3: dv = dv.rearrange("p (a b) -> p a b", b=shp[2])
            K.cp("dve", dv, ap, R=list(R), W=[dbgt.d])
            K.dma("sp", dbgq, o, dbgt[0:p, 0:n], R=[dbgt.d])

        q_misc = K.dslot("misc")
        vec = [sb([128, NV]) for _ in range(L)]
        wa2b = [sb([128, DR], BF16) for _ in range(L)]; g2b = [sb([128, DR], BF16) for _ in range(L)]
        modv = [sb([128, 6, 8, 3]) for _ in range(L)]
        for l in range(L):
            K.dma("sp", q_misc, vec[l][:, 0:NV_IN], vecs[l], W=[vec[l].d])
            K.dma("pool", q_misc, wa2b[l][:, :], wa2[l], W=[wa2b[l].d])
            K.dma("pool", q_misc, g2b[l][:, :], g2[l], W=[g2b[l].d])
            V = vec[l]
            vc = lambda n, a=0, k=1: V[:, VOFF[n] + a:VOFF[n] + a + k]
            K.ts("dve", vc("omu", 0, 14), vc("mu", 0, 14), -1.0, 1.0, ALU.mult, ALU.add, W=[V.d])
            K.ts("dve", vc("hw0", 0, 4), vc("w0", 0, 4), 0.5, None, ALU.mult, W=[V.d])
            K.ts("dve", vc("ha0", 0, 4), vc("a0", 0, 4), 0.5, None, ALU.mult, W=[V.d])
            K.ts("dve", vc("omka", 0, 4), vc("k_a", 0, 4), -1.0, 1.0, ALU.mult, ALU.add, W=[V.d])
            K.ts("dve", vc("hlg", 0, 4), vc("conv_ln_g", 0, 4), 0.5, None, ALU.mult, W=[V.d])
            K.ts("dve", vc("hlb", 0, 4), vc("conv_ln_b", 0, 4), 0.5, None, ALU.mult, W=[V.d])

        with contextlib.ExitStack() as ph:
            cf = sb([128, 8, 3], es=ph); ct = sb([128, 8, 3], es=ph); scb = sb([128, 8, 3], BF16, es=ph)
            wm = [sb([128, 6 * D], BF16, es=ph) for _ in range(2)]
            q_wm = [K.dslot("wm0"), K.dslot("wm1")]
            K.dma("sp", q_misc, cf[:, :, :], cT, W=[cf.d])
            K.act(ct[:, :, :], cf[:, :, :], AF.Tanh, scale=0.5, R=[cf.d], W=[ct.d])
            K.ts("dve", ct[:, :, :], ct[:, :, :], 0.5, 0.5, ALU.mult, ALU.add, W=[ct.d])
            K.tt("dve", scb[:, :, :], ct[:, :, :], cf[:, :, :], ALU.mult, R=[cf.d, ct.d], W=[scb.d])
            i = 0
            for l in range(L):
                b = bank()
                for kc in range(8):
                    w = wm[i % 2]
                    K.dma("pool", q_wm[i % 2], w[:, :], wmod[l, kc * 128:(kc + 1) * 128, :], W=[w.d], max_dma_last_dim=4096)
                    for n in range(48):
                        K.mm(b.t[:, n * 3:n * 3 + 3], w[:, n * 128:(n + 1) * 128], scb[:, kc, :], start=(kc == 0 and n == 0), stop=(kc == 7),
                             R=[w.d, scb.d], W=[b.d], skip=True)
                    i += 1
                V = vec[l]; M = modv[l]
                K.tt("dve", M[:, :, :, :].rearrange("p a c s -> p (a c) s"), b.t[:, 0:144].rearrange("p (n s) -> p n s", s=3),
                     bc(V[:, VOFF["b_mod"]:VOFF["b_mod"] + 48], 2, 3), ALU.add, R=[V.d], W=[b.d, M.d])
                rel(b)
                for gi, gn in [(1, "g_mix_pre"), (2, "g_mix_post"), (4, "g_ffn_pre"), (5, "g_ffn_post")]:
                    K.stt(M[:, gi, :, :], M[:, gi, :, :], 1.0, bc(V[:, VOFF[gn]:VOFF[gn] + 8], 2, 3), ALU.add, ALU.mult, R=[V.d], W=[M.d])
            if DEBUG:
                dbg_on[0] = True
                for nm_, t_ in [("blk", blk), ("mts", mts), ("ms", ms), ("mti", mti), ("ident", ident)]: dbg(nm_, t_[:, :], R=[cst])
                dbg("modv", modv[0][:, :, :, :].rearrange("p a c s -> p (a c s)"), R=[modv[0].d]); dbg_on[0] = False
            K.barrier()

        scr_dep = {}
        for l in range(L):
            for n, ns, wd in WSPEC:
                d = Dep(); scr_dep[(n, l)] = d; q_cv = K.dslot("cv")
                if wd == 1024:
                    K.dma("pool", q_cv, wscr[n][l].rearrange("s p w -> (s p) w"), wsrc[n][l].rearrange("s p w -> (s p) w"), W=[d])
                else:
                    K.dma("pool", q_cv, wscr[n][l].rearrange("s p (h w) -> (s p h) w", h=2),
                          wsrc[n][l].rearrange("s p (h w) -> (s p h) w", h=2), W=[d])

        x = sb([128, 8, 512], nd=8)
        hT = sb([128, 8, 512], BF16, nd=8); mixT = sb([128, 8, 512], BF16, nd=8); rstd = sb([128, 512])
        stage = sb([128, 1024]); q_stage = K.dslot("stage")
        class Set: pass
        PS = Set(); PS.W = 2 * TT; PS.segs = [(0, TT, 0), (TT, TT, 1)]; PS.ns = 2; PS.tt = TT
        SS = Set(); SS.W = 64; SS.segs = [(0, 64, 2)]; SS.ns = 1; SS.tt = 64
        for S in (PS, SS):
            S.full = [sb([128, 4, S.ns, CB + S.tt], BF16, nd=4) for _ in range(L)]
            S.zh = [sb([128, 14, S.ns], nd=14) for _ in range(L)]
            S.ST = [sb([128, 4, S.ns, HD], nd=4) for _ in range(L)]; S.STb = [sb([128, 4, S.ns, 2, HD], BF16) for _ in range(L)]
            S.gl = sb([128, 4, S.ns, CB]); S.zl = sb([128, 14, S.ns])
        st = Dep()
        for l in range(L):
            K.op("pool", lambda e: e.memset(PS.full[l][:, :, :, 0:CB], 0.0), W=PS.full[l].ds)
            K.op("pool", lambda e: e.memset(PS.zh[l][:, :, :], 0.0), W=PS.zh[l].ds)
            K.op("pool", lambda e: e.memset(PS.ST[l][:, :, :, :], 0.0), W=PS.ST[l].ds)
            K.op("pool", lambda e: e.memset(PS.STb[l][:, :, :, :, :], 0.0), W=[PS.STb[l].d])
            K.op("pool", lambda e: e.memset(SS.STb[l][:, :, :, :, :], 0.0), W=[SS.STb[l].d])
            K.dma("sp", q_stage, stage[0:CB, 0:DC], cconv[l], W=[stage.d])
            b = bank()
            for c in range(4):
                K.tr(b.t[:, c * 32:c * 32 + CB], stage[0:CB, c * 128:(c + 1) * 128], identf[0:CB, 0:CB], R=[stage.d, cst], W=[b.d])
            K.cp("dve", SS.full[l][:, :, 0, 0:CB], b.t[:, 0:128].rearrange("p (c t) -> p c t", t=32)[:, :, 0:CB], W=[b.d] + SS.full[l].ds); rel(b)
            K.dma("sp", q_stage, stage[0:14, 0:128], sshift[l], W=[stage.d])
            b = bank()
            K.tr(b.t[:, 0:14], stage[0:14, 0:128], identf[0:14, 0:14], R=[stage.d, cst], W=[b.d])
            K.tt("dve", SS.zh[l][:, :, 0], b.t[:, 0:14], vec[l][:, VOFF["mu"]:VOFF["mu"] + 14], ALU.mult, R=[vec[l].d], W=[b.d] + SS.zh[l].ds); rel(b)
            K.dma("sp", q_stage, stage[0:64, 0:512].rearrange("v (h k) -> v h k", k=64), swkv[l].rearrange("h v k -> v h k"), W=[stage.d])
            b = bank()
            for c in range(4):
                K.tr(b.t[:, c * 64:(c + 1) * 64], stage[0:64, c * 128:(c + 1) * 128], identf[0:64, 0:64], R=[stage.d, cst], W=[b.d])
            K.cp("dve", SS.ST[l][:, :, 0, :], b.t[:, 0:256].rearrange("p (c v) -> p c v", v=64), W=[b.d] + SS.ST[l].ds)
            K.cp("act", SS.STb[l][0:64, :, 0, 0, :], b.t[0:64, 0:256].rearrange("p (c v) -> p c v", v=64), W=[b.d, SS.STb[l].d])
            K.cp("act", SS.STb[l][64:128, :, 0, 1, :], b.t[64:128, 0:256].rearrange("p (c v) -> p c v", v=64), W=[b.d, SS.STb[l].d]); rel(b)

        NR = 8
        slabs = [sb([128, 8, 128], BF16) for _ in range(NR)]; slabq = [K.dslot("sl%d" % i) for i in range(NR)]
        dslabs = [sb([128, 22, 128], BF16) for _ in range(2)]; dslabq = [K.dslot("dsl%d" % i) for i in range(2)]
        passes = ["s"] + list(range(NPASS))
        sched = []; dsched = []
        for _ in passes:
            for l in range(L):
                sched += [("w_in", l, j) for j in WIN_ORDER] + [("w_out", l, j) for j in range(8)]
                for j in range(22): sched += [("w_gate", l, j), ("w_up", l, j)]
                dsched += [("w_down", l, j) for j in range(8)]
        sp_ = [0, 0]; dp_ = [0, 0]

        def next_slab(expect):
            i = sp_[1]; assert sched[i] == expect, (sched[i], expect)
            while sp_[0] < min(len(sched), i + NR):
                n, l, j = sched[sp_[0]]; s = slabs[sp_[0] % NR]
                K.dma("sp", slabq[sp_[0] % NR], s[:, :, :].rearrange("p k c -> p (k c)"), wscr[n][l, j], R=[scr_dep[(n, l)]], W=[s.d])
                sp_[0] += 1
            sp_[1] += 1
            return slabs[i % NR]

        def next_dslab(expect):
            i = dp_[1]; assert dsched[i] == expect
            while dp_[0] < min(len(dsched), i + 2):
                n, l, j = dsched[dp_[0]]; s = dslabs[dp_[0] % 2]
                K.dma("sp", dslabq[dp_[0] % 2], s[:, :, :].rearrange("p k c -> p (k c)"), wscr[n][l, j], R=[scr_dep[(n, l)]], W=[s.d])
                dp_[0] += 1
            dp_[1] += 1
            return dslabs[i % 2]

        def phase_bufs(ph):
            if not hasattr(ph, "sq"):
                ph.sq = sb([128, 8, 512], BF16, es=ph, nd=8); ph.tmp = [sb([128, 256], es=ph) for _ in range(3)]
            return ph.sq

        def rms_stats(S, src, ph):
            W = S.W
            sq = phase_bufs(ph)
            b = bank()
            for c in range(8):
                if src is not None:
                    K.act(sq[:, c, 0:W], src[:, c, 0:W], AF.Square, R=[src.ds[c]], W=[sq.ds[c]])
                K.mm(b.t[:, 0:W], ones[:, :], sq[:, c, 0:W], start=(c == 0), stop=(c == 7), R=[sq.ds[c], cst], W=[b.d])
            K.act(rstd[:, 0:W], b.t[:, 0:W], AF.Ln, scale=1.0 / D, bias=RMS_EPS, W=[b.d, rstd.d]); rel(b)
            K.act(rstd[:, 0:W], rstd[:, 0:W], AF.Exp, scale=-0.5, W=[rstd.d])

        def norm_mod(S, l, gi, si, ph):
            rms_stats(S, x, ph)
            tmp = ph.tmp
            i = 0
            for (c0, n, s) in S.segs:
                for c in range(8):
                    t = tmp[i % 3]; i += 1
                    K.tt("dve", t[:, 0:n], x[:, c, c0:c0 + n], rstd[:, c0:c0 + n], ALU.mult, R=[x.ds[c], rstd.d], W=[t.d])
                    K.act(hT[:, c, c0:c0 + n], t[:, 0:n], AF.Identity, scale=modv[l][:, gi, c, s:s + 1], bias=modv[l][:, si, c, s:s + 1],
                          R=[t.d, modv[l].d], W=[hT.ds[c]])
            if l == 0: dbg("hT%d" % gi, hT[:, :, 0:S.W], R=hT.ds); dbg("rstd%d" % gi, rstd[:, 0:S.W], R=[rstd.d])

        def post_norm(S, l, m, gai, ph):
            rms_stats(S, None, ph)
            tmp = ph.tmp
            jobs = [(c0, n, s, c) for (c0, n, s) in S.segs for c in range(8)]

            def first(i):
                c0, n, s, c = jobs[i]; t = tmp[i % 3]
                K.tt("dve", t[:, 0:n], m[:, c, c0:c0 + n], rstd[:, c0:c0 + n], ALU.mult, R=[m.ds[c], rstd.d], W=[t.d])

            def second(i):
                c0, n, s, c = jobs[i]; t = tmp[i % 3]
                K.stt(x[:, c, c0:c0 + n], t[:, 0:n], modv[l][:, gai, c, s:s + 1], x[:, c, c0:c0 + n], ALU.mult, ALU.add,
                      R=[t.d, modv[l].d], W=[x.ds[c]])
            first(0)
            for i in range(len(jobs)):
                if i + 1 < len(jobs): first(i + 1)
                second(i)

        def proj(S, slab, src, nk=8):
            b = bank()
            for kc in range(nk):
                K.mm(b.t[:, 0:S.W], slab[:, kc, :], src[:, kc, 0:S.W], start=(kc == 0), stop=(kc == nk - 1), R=[slab.d, src.ds[kc]], W=[b.d])
            return b

        def seg3(S, ap2d):
            return ap2d.rearrange("p (s t) -> p s t", s=S.ns)

        def load_x(S, p):
            for (c0, n, s) in S.segs:
                for blk0 in range(0, n, 128):
                    nb = min(128, n - blk0)
                    src = xs[blk0:blk0 + nb, :] if s == 2 else xp[s, p * TT + blk0:p * TT + blk0 + nb, :]
                    K.dma("sp", q_stage, stage[0:nb, :], src, W=[stage.d])
                    for half in range(2):
                        b = bank()
                        for c4 in range(4):
                            c = half * 4 + c4
                            K.tr(b.t[:, c4 * 128:c4 * 128 + nb], stage[0:nb, c * 128:(c + 1) * 128], identf[0:nb, 0:nb], R=[stage.d, cst], W=[b.d])
                        K.cp("act" if half else "dve", x[:, half * 4:half * 4 + 4, c0 + blk0:c0 + blk0 + nb],
                             b.t[:, :].rearrange("p (c t) -> p c t", t=128)[:, :, 0:nb], W=[b.d] + x.ds[half * 4:half * 4 + 4]); rel(b)

        def store_x(S, p):
            for (c0, n, s) in S.segs:
                for blk0 in range(0, n, 128):
                    nb = min(128, n - blk0)
                    for half in range(2):
                        b = bank()
                        for c4 in range(4):
                            c = half * 4 + c4
                            K.tr(b.t[0:nb, c4 * 128:(c4 + 1) * 128], x[:, c, c0 + blk0:c0 + blk0 + nb], identf[:, :], R=[x.ds[c], cst], W=[b.d])
                        K.cp("act" if half else "dve", stage[0:nb, half * 512:(half + 1) * 512], b.t[0:nb, :], W=[b.d, stage.d]); rel(b)
                    dst = ys[blk0:blk0 + nb, :] if s == 2 else yp[s, p * TT + blk0:p * TT + blk0 + nb, :]
                    K.dma("sp", q_stage, dst, stage[0:nb, :], R=[stage.d])

        def conv_phase(S, l, last):
            W = S.W; V = vec[l]; full = S.full[l]; n = S.tt
            vc = lambda nm, a=0, k=1: V[:, VOFF[nm] + a:VOFF[nm] + a + k]
            with contextlib.ExitStack() as ph:
                K.barrier()
                norm_mod(S, l, 1, 0, ph)
                tgs = [sb([128, 512], es=ph) for _ in range(2)]; ycv = sb([128, 4, 512], es=ph, nd=4); ysq = sb([128, 4, 512], BF16, es=ph, nd=4)
                ybf = sb([128, 4, 512], BF16, es=ph, nd=4)
                diags = [sb([128, CW, 128], BF16, es=ph) for _ in range(2)]; mean = sb([128, 512], es=ph); var = sb([128, 512], es=ph)
                t1 = sb([128, 512], es=ph); t2 = sb([128, 512], es=ph)

                def glu_c(c):
                    tg = tgs[c % 2]; diag = diags[c % 2]
                    K.tt("dve", diag[:, :, :], bc(ident[:, :], 1, CW), vc("conv_dw", c * CW, CW).unsqueeze(2).broadcast_to([128, CW, 128]), ALU.mult,
                         R=[cst, V.d], W=[diag.d])
                    bg = proj(S, next_slab(("w_in", l, 4 + c)), hT)
                    K.act(tg[:, 0:W], bg.t[:, 0:W], AF.Tanh, scale=0.5, W=[bg.d, tg.d]); rel(bg)
                    K.act(tg[:, 0:W], tg[:, 0:W], AF.Identity, scale=0.5, bias=0.5, W=[tg.d])
                    ba = proj(S, next_slab(("w_in", l, c)), hT)
                    if last:
                        K.tt("dve", S.gl[:, c, :, :], seg3(S, ba.t[:, 0:W])[:, :, n - CB:n], seg3(S, tg[:, 0:W])[:, :, n - CB:n], ALU.mult,
                             R=[tg.d], W=[ba.d, S.gl.d])
                    K.tt("dve", full[:, c, :, CB:CB + n], seg3(S, ba.t[:, 0:W]), seg3(S, tg[:, 0:W]), ALU.mult, R=[tg.d], W=[ba.d, full.ds[c]]); rel(ba)

                def conv_c(c):
                    diag = diags[c % 2]
                    by = bank()
                    for j in range(CW):
                        K.mm(by.t[:, 0:W], diag[:, j, :], full[:, c, :, j:j + n], start=(j == 0), stop=(j == CW - 1), R=[diag.d, full.ds[c]], W=[by.d])
                    K.act(ycv[:, c, 0:W], by.t[:, 0:W], AF.Identity, bias=vc("conv_b", c), R=[V.d], W=[by.d, ycv.ds[c]])
                    K.act(ysq[:, c, 0:W], by.t[:, 0:W], AF.Square, bias=vc("conv_b", c), R=[V.d], W=[by.d, ysq.ds[c]])
                    K.act(ybf[:, c, 0:W], by.t[:, 0:W], AF.Identity, bias=vc("conv_b", c), R=[V.d], W=[by.d, ybf.ds[c]]); rel(by)
                    if l == 0 and c == 0: dbg("full0", full[:, 0, 0, :], R=full.ds); dbg("ycv0", ycv[:, 0, 0:W], R=ycv.ds)
                    K.cp("pool", full[:, c, :, 0:CB], full[:, c, :, n:n + CB], W=[full.ds[c]])
                glu_c(0)
                for c in range(4):
                    if c < 3: glu_c(c + 1)
                    conv_c(c)
                b1 = bank()
                for c in range(4):
                    K.mm(b1.t[:, 0:W], ones[:, :], ybf[:, c, 0:W], start=(c == 0), stop=(c == 3), R=[ybf.ds[c], cst], W=[b1.d])
                K.ts("dve", mean[:, 0:W], b1.t[:, 0:W], 1.0 / DC, None, ALU.mult, W=[b1.d, mean.d]); rel(b1)
                b2 = bank()
                for c in range(4):
                    K.mm(b2.t[:, 0:W], ones[:, :], ysq[:, c, 0:W], start=(c == 0), stop=(c == 3), R=[ysq.ds[c], cst], W=[b2.d])
                K.tt("dve", var[:, 0:W], mean[:, 0:W], mean[:, 0:W], ALU.mult, R=[mean.d], W=[var.d])
                K.stt(var[:, 0:W], b2.t[:, 0:W], 1.0 / DC, var[:, 0:W], ALU.mult, ALU.subtract, W=[b2.d, var.d]); rel(b2)
                K.act(var[:, 0:W], var[:, 0:W], AF.Ln, bias=LN_EPS, W=[var.d])
                K.act(var[:, 0:W], var[:, 0:W], AF.Exp, scale=-0.5, W=[var.d])
                t1s = [t1] + [sb([128, 512], es=ph) for _ in range(3)]; t2s = [t2] + [sb([128, 512], es=ph) for _ in range(3)]
                for c in range(4):
                    K.tt("dve", t1s[c][:, 0:W], ycv[:, c, 0:W], mean[:, 0:W], ALU.subtract, R=[ycv.ds[c], mean.d], W=[t1s[c].d])
                for c in range(4):
                    K.tt("dve", t1s[c][:, 0:W], t1s[c][:, 0:W], var[:, 0:W], ALU.mult, R=[var.d], W=[t1s[c].d])
                for c in range(4):
                    K.act(t2s[c][:, 0:W], t1s[c][:, 0:W], AF.Tanh, scale=vc("hlg", c), bias=vc("hlb", c), R=[t1s[c].d, V.d], W=[t2s[c].d])
                for c in range(4):
                    K.ts("dve", t1s[c][:, 0:W], t1s[c][:, 0:W], vc("hlg", c), vc("hlb", c), ALU.mult, ALU.add, R=[V.d], W=[t1s[c].d])
                for c in range(4):
                    K.stt(mixT[:, c, 0:W], t2s[c][:, 0:W], 1.0, t1s[c][:, 0:W], ALU.add, ALU.mult, R=[t1s[c].d, t2s[c].d], W=[mixT.ds[c]])
                if l == 0: dbg("mean", mean[:, 0:W], R=[mean.d]); dbg("crstd", var[:, 0:W], R=[var.d]); dbg("mixc", mixT[:, 0:4, 0:W], R=mixT.ds)

        def rwkv_phase(S, l, last):
            W = S.W; V = vec[l]; n = S.tt; ns = S.ns; zh = S.zh[l]
            SUB = min(128, W); NSUB = W // SUB
            CH = SUB; NCH = W // CH; CPS = 1; NLEV = 6 if CH == 128 else 5
            vc = lambda nm, a=0, k=1: V[:, VOFF[nm] + a:VOFF[nm] + a + k]
            with contextlib.ExitStack() as ph:
                K.barrier()
                def pad(src, dst, c):
                    K.cp("pool", dst[0:64, c, 0, 0:W], src[0:64, c, 0:W], R=[src.ds[c]], W=[dst.ds[c]])
                    K.cp("pool", dst[64:128, c, 1, 0:W], src[64:128, c, 0:W], R=[src.ds[c]], W=[dst.ds[c]])
                g4 = sb([128, 4, 512], BF16, es=ph, nd=4); bv = sb([128, 4, 512], BF16, es=ph, nd=4)
                ah, bt, kt, rh, vb = [sb([128, 4, 512], BF16, es=ph, nd=4) for _ in range(5)]
                bhp, khp = [sb([128, 4, 2, 512], BF16, es=ph, nd=8) for _ in range(2)]
                WC = sb([128, 4, 8], es=ph, nd=4)
                ph1 = contextlib.ExitStack()
                f = lambda dt=F32: sb([128, 512], dt, es=ph1)
                lwin = f(BF16); sg = f(BF16); zm0 = f()
                for pz in (bhp, khp):
                    K.op("pool", lambda e, pz=pz: e.memset(pz[:, :, :, :], 0.0), W=pz.ds)

                class TS: pass
                tsets = []
                for _ in range(2):
                    t = TS(); tsets.append(t)
                    t.zt = sb([128, 2, 257], es=ph1, nd=2)
                    t.zm, t.lw, t.Lc, t.aa, t.k0, t.kf, t.rs = [f() for _ in range(7)]
                    t.eL, t.enL, t.eLm, t.eCL, t.tb, t.kr, t.bh, t.kh = [f(BF16) for _ in range(8)]

                def pad(src, dst, c):
                    K.cp("pool", dst[0:64, c, 0, 0:W], src[0:64, 0:W], R=[src.d], W=[dst.ds[2 * c]])
                    K.cp("pool", dst[64:128, c, 1, 0:W], src[64:128, 0:W], R=[src.d], W=[dst.ds[2 * c + 1]])

                def mixz(j, zm, zt):
                    zi = j - 8
                    b = proj(S, next_slab(("w_in", l, j)), hT)
                    z3 = seg3(S, b.t[:, 0:W])
                    K.cp("act", zt[:, 0:ns, 0:1], zh[:, zi, :].unsqueeze(2), R=[zh.ds[zi]], W=[zt.ds[1]])
                    K.act(zt[:, 0:ns, 1:n + 1], z3, AF.Identity, scale=vc("mu", zi), R=[V.d], W=[b.d, zt.ds[0]])
                    K.act(zh[:, zi, :].unsqueeze(2), z3[:, :, n - 1:n], AF.Identity, scale=vc("mu", zi), R=[V.d], W=[b.d, zh.ds[zi]])
                    if last:
                        K.cp("act", S.zl[:, zi, :].unsqueeze(2), z3[:, :, n - 1:n], W=[b.d, S.zl.d])
                    K.stt(seg3(S, zm[:, 0:W]), z3, vc("omu", zi), zt[:, 0:ns, 0:n], ALU.mult, ALU.add, R=[V.d] + zt.ds, W=[b.d, zm.d]); rel(b)

                mixz(20, zm0, tsets[0].zt)
                if l == 0: dbg("zm20", zm0[:, 0:W], R=[zm0.d])
                K.act(lwin[0:64, 0:W], zm0[0:64, 0:W], AF.Tanh, R=[zm0.d], W=[lwin.d])
                K.cp("act", lwin[64:128, 0:W], zm0[64:128, 0:W], R=[zm0.d], W=[lwin.d])
                mixz(21, zm0, tsets[1].zt)
                K.act(sg[:, 0:W], zm0[:, 0:W], AF.Tanh, scale=0.5, R=[zm0.d], W=[sg.d])
                K.act(sg[:, 0:W], sg[:, 0:W], AF.Identity, scale=0.5, bias=0.5, W=[sg.d])

                def prep_ops(c, t):
                    o = []; add = o.append
                    cs = slice(c * 128, (c + 1) * 128)
                    zm, lw, Lc, aa, k0, kf, rs = t.zm, t.lw, t.Lc, t.aa, t.k0, t.kf, t.rs
                    eL, enL, eLm, eCL, tb, kr = t.eL, t.enL, t.eLm, t.eCL, t.tb, t.kr
                    hold = {}

                    def lora(lhsT, rhs, deps, fin):
                        b = bank()
                        K.mm(b.t[:, 0:W], lhsT, rhs, R=deps, W=[b.d])
                        fin(b); rel(b)
                    add(lambda: lora(wa2b[l][0:64, cs], lwin[0:64, 0:W], [wa2b[l].d, lwin.d],
                                     lambda b: K.act(lw[:, 0:W], b.t[:, 0:W], AF.Tanh, scale=0.5, bias=vc("hw0", c), R=[V.d], W=[b.d, lw.d])))
                    add(lambda: K.act(lw[:, 0:W], lw[:, 0:W], AF.Identity, scale=-0.5 * EM05, bias=-0.5 * EM05, W=[lw.d]))
                    add(lambda: lora(wa2b[l][64:128, cs], lwin[64:128, 0:W], [wa2b[l].d, lwin.d],
                                     lambda b: K.act(aa[:, 0:W], b.t[:, 0:W], AF.Tanh, scale=0.5, bias=vc("ha0", c), R=[V.d], W=[b.d, aa.d])))
                    add(lambda: K.act(aa[:, 0:W], aa[:, 0:W], AF.Identity, scale=0.5, bias=0.5, W=[aa.d]))
                    add(lambda: lora(g2b[l][:, cs], sg[:, 0:W], [g2b[l].d, sg.d],
                                     lambda b: K.cp("act", g4[:, c, 0:W], b.t[:, 0:W], W=[b.d, g4.ds[c]])))
                    add(lambda: K.op("dve", lambda e: e.tensor_tensor_scan(out=Lc[:, 0:W], data0=rmask[:, 0:W], data1=lw[:, 0:W], initial=0.0, op0=ALU.mult, op1=ALU.add),
                                     R=[lw.d, cst], W=[Lc.d]))
                    add(lambda: K.tt("dve", lw[:, 0:W], Lc[:, 0:W], lw[:, 0:W], ALU.subtract, R=[Lc.d], W=[lw.d]))
                    add(lambda: K.act(eL[:, 0:W], Lc[:, 0:W], AF.Exp, R=[Lc.d], W=[eL.d]))
                    add(lambda: K.act(enL[:, 0:W], Lc[:, 0:W], AF.Exp, scale=-1.0, R=[Lc.d], W=[enL.d]))
                    add(lambda: K.act(eLm[:, 0:W], lw[:, 0:W], AF.Exp, R=[lw.d], W=[eLm.d]))
                    L3 = Lc[:, 0:W].rearrange("p (c t) -> p c t", t=CH)
                    add(lambda: K.act(WC[:, c, 0:NCH].unsqueeze(2), L3[:, :, CH - 1:CH], AF.Exp, R=[Lc.d], W=[WC.ds[c]]))
                    add(lambda: K.tt("dve", lw[:, 0:W].rearrange("p (c t) -> p c t", t=CH), L3[:, :, CH - 1:CH].broadcast_to([128, NCH, CH]), L3, ALU.subtract,
                                     R=[Lc.d, eLm.d], W=[lw.d]))
                    add(lambda: K.act(eCL[:, 0:W], lw[:, 0:W], AF.Exp, R=[lw.d], W=[eCL.d]))
                    add(lambda: mixz(12 + c, zm, t.zt))
                    add(lambda: K.act(k0[:, 0:W], zm[:, 0:W], AF.Identity, scale=vc("k_k", c), R=[zm.d, V.d], W=[k0.d]))
                    add(lambda: K.act(tb[:, 0:W], k0[:, 0:W], AF.Square, R=[k0.d], W=[tb.d]))
                    add(lambda: lora(blk[:, :], tb[:, 0:W], [tb.d, cst],
                                     lambda b: K.act(rs[:, 0:W], b.t[:, 0:W], AF.Ln, bias=L2_EPS, W=[b.d, rs.d])))
                    add(lambda: K.act(rs[:, 0:W], rs[:, 0:W], AF.Exp, scale=-0.5, W=[rs.d]))
                    add(lambda: K.tt("dve", k0[:, 0:W], k0[:, 0:W], rs[:, 0:W], ALU.mult, R=[rs.d], W=[k0.d]))
                    add(lambda: K.act(kf[:, 0:W], aa[:, 0:W], AF.Identity, scale=vc("k_a", c), bias=vc("omka", c), R=[aa.d, V.d], W=[kf.d]))
                    add(lambda: K.tt("dve", kf[:, 0:W], kf[:, 0:W], zm[:, 0:W], ALU.mult, R=[zm.d], W=[kf.d]))
                    add(lambda: K.stt(ah[:, c, 0:W], k0[:, 0:W], -1.0, eLm[:, 0:W], ALU.mult, ALU.mult, R=[k0.d, eLm.d], W=[ah.ds[c]]))
                    add(lambda: K.tt("dve", rs[:, 0:W], k0[:, 0:W], aa[:, 0:W], ALU.mult, R=[k0.d, aa.d], W=[rs.d]))
                    add(lambda: K.tt("dve", t.bh[:, 0:W], rs[:, 0:W], enL[:, 0:W], ALU.mult, R=[rs.d, enL.d], W=[t.bh.d]))
                    add(lambda: pad(t.bh, bhp, c))
                    add(lambda: K.tt("dve", t.kh[:, 0:W], kf[:, 0:W], enL[:, 0:W], ALU.mult, R=[kf.d, enL.d], W=[t.kh.d]))
                    add(lambda: pad(t.kh, khp, c))
                    add(lambda: K.tt("pool", bt[:, c, 0:W], rs[:, 0:W], eCL[:, 0:W], ALU.mult, R=[rs.d, eCL.d], W=[bt.ds[c]]))
                    add(lambda: K.tt("pool", kt[:, c, 0:W], kf[:, 0:W], eCL[:, 0:W], ALU.mult, R=[kf.d, eCL.d], W=[kt.ds[c]]))
                    add(lambda: K.act(kr[:, 0:W], kf[:, 0:W], AF.Identity, scale=vc("r_k", c), R=[kf.d, V.d], W=[kr.d]))
                    add(lambda: mixz(8 + c, zm, t.zt))
                    add(lambda: K.tt("dve", rh[:, c, 0:W], zm[:, 0:W], eL[:, 0:W], ALU.mult, R=[zm.d, eL.d], W=[rh.ds[c]]))
                    add(lambda: K.tt("dve", tb[:, 0:W], zm[:, 0:W], kr[:, 0:W], ALU.mult, R=[zm.d, kr.d], W=[tb.d]))

                    def bsum():
                        hold["bb"] = bb = bank()
                        K.mm(bb.t[:, 0:W], blk[:, :], tb[:, 0:W], R=[tb.d, cst], W=[bb.d])
                    add(bsum)
                    add(lambda: mixz(16 + c, zm, t.zt))
                    add(lambda: K.cp("act", vb[:, c, 0:W], zm[:, 0:W], R=[zm.d], W=[vb.ds[c]]))

                    def bvf():
                        bb = hold["bb"]
                        K.tt("dve", bv[:, c, 0:W], bb.t[:, 0:W], zm[:, 0:W], ALU.mult, R=[zm.d], W=[bb.d, bv.ds[c]]); rel(bb)
                    add(bvf)
                    return o

                for c0 in (0, 2):
                    oa = prep_ops(c0, tsets[0]); ob = prep_ops(c0 + 1, tsets[1])
                    for fa, fb in zip(oa, ob):
                        fa(); fb()

                ph1.close()
                K.barrier()
                ST = S.ST[l]; STb = S.STb[l]

                def core_bufs():
                    A = lambda: sb([128, 8, 128], BF16, es=ph, nd=2)
                    mats = [A() for _ in range(9)]
                    vts = [sb([128, 512], BF16, es=ph) for _ in range(3)]
                    Xs = sb([128, 512], BF16, es=ph); Us = sb([128, 512], BF16, es=ph)
                    ysq = sb([128, 512], es=ph); yn = sb([128, 512], BF16, es=ph)
                    st8 = [sb([128, 8], es=ph) for _ in range(4)]; o1 = [sb([128, 128], es=ph) for _ in range(4)]
                    return (mats, vts, Xs, Us, ysq, yn, st8, o1)
                bufsets = [core_bufs() for _ in range(2 if NSUB > 1 else 1)]

                def core_body(q, B, yb):
                    (AakT, ArbT, ArkT, Pm, Ak, AkT, Ak2, AkT2, Pt), (vT, btT, ktT), Xs, Us, ysq, yn, st8, o1s_ = B
                    o1s = o1s_
                    q0 = q * SUB; qs = slice(q0, q0 + SUB); seg = q0 // n
                    for src, dst in [(vb, vT), (bt, btT), (kt, ktT)]:
                        b = bank(); pb = b.t[:, :].bitcast(BF16)
                        for c in range(4):
                            K.tr(pb[0:SUB, c * 128:(c + 1) * 128], src[:, c, qs], ident[:, :], R=[src.ds[c], cst], W=[b.d])
                        K.cp("act", dst[0:SUB, :], pb[0:SUB, 0:512], W=[b.d, dst.d]); rel(b)

                    def amat(lh, lpad, rh_, rpad, mask, dst):
                        for hb in range(2):
                            b = bank()
                            for h4 in range(4):
                                h = hb * 4 + h4; c = h // 2
                                la = lh[:, c, h % 2, qs] if lpad else lh[:, c, qs]
                                ra = rh_[:, c, h % 2, qs] if rpad else rh_[:, c, qs]
                                K.mm(b.t[0:SUB, h4 * 128:h4 * 128 + SUB], la, ra, R=[lh.ds[h] if lpad else lh.ds[c], rh_.ds[h] if rpad else rh_.ds[c]], W=[b.d])
                            K.tt("dve", dst[0:SUB, hb * 4:hb * 4 + 4, 0:SUB], b.t[0:SUB, :].rearrange("p (h t) -> p h t", t=128)[:, :, 0:SUB],
                                 bc(mask[0:SUB, 0:SUB], 1, 4), ALU.mult, R=[cst], W=[b.d, dst.ds[hb]]); rel(b)
                    amat(bhp, 1, ah, 0, mts, AkT)
                    amat(ah, 0, bhp, 1, ms, Ak)
                    amat(khp, 1, ah, 0, mts, AakT)
                    amat(bhp, 1, rh, 0, mti, ArbT)
                    amat(khp, 1, rh, 0, mti, ArkT)
                    K.tt("pool", Pm[0:SUB, :, 0:SUB], AkT[0:SUB, :, 0:SUB], bc(ident[0:SUB, 0:SUB], 1, 8), ALU.add, R=AkT.ds + [cst], W=Pm.ds)
                    ca, cat, na, nat = Ak, AkT, Ak2, AkT2
                    for lev in range(NLEV):
                        for hb in range(2):
                            b = bank()
                            for h4 in range(4):
                                h = hb * 4 + h4
                                K.mm(b.t[0:SUB, h4 * 128:h4 * 128 + SUB], cat[0:SUB, h, 0:SUB], ca[0:SUB, h, 0:SUB], R=[cat.ds[hb], ca.ds[hb]], W=[b.d])
                            K.cp("act", na[0:SUB, hb * 4:hb * 4 + 4, 0:SUB], b.t[0:SUB, :].rearrange("p (h t) -> p h t", t=128)[:, :, 0:SUB], W=[b.d, na.ds[hb]]); rel(b)
                        if lev < NLEV - 1:
                            for hb in range(2):
                                b = bank()
                                for h4 in range(4):
                                    h = hb * 4 + h4
                                    K.mm(b.t[0:SUB, h4 * 128:h4 * 128 + SUB], ca[0:SUB, h, 0:SUB], cat[0:SUB, h, 0:SUB], R=[cat.ds[hb], ca.ds[hb]], W=[b.d])
                                K.cp("act", nat[0:SUB, hb * 4:hb * 4 + 4, 0:SUB], b.t[0:SUB, :].rearrange("p (h t) -> p h t", t=128)[:, :, 0:SUB], W=[b.d, nat.ds[hb]]); rel(b)
                        for hb in range(2):
                            b = bank()
                            for h4 in range(4):
                                h = hb * 4 + h4
                                K.mm(b.t[0:SUB, h4 * 128:h4 * 128 + SUB], na[0:SUB, h, 0:SUB], Pm[0:SUB, h, 0:SUB], R=[na.ds[hb], Pm.ds[hb]], W=[b.d])
                            K.tt("dve", Pt[0:SUB, hb * 4:hb * 4 + 4, 0:SUB], b.t[0:SUB, :].rearrange("p (h t) -> p h t", t=128)[:, :, 0:SUB],
                                 Pm[0:SUB, hb * 4:hb * 4 + 4, 0:SUB], ALU.add, R=[Pm.ds[hb]], W=[b.d, Pt.ds[hb]]); rel(b)
                        Pm, Pt = Pt, Pm
                        ca, cat, na, nat = na, nat, ca, cat
                    if l == 0 and q == 0:
                        dbg("vT", vT[0:SUB, :], R=[vT.d]); dbg("AabT", ca[0:SUB, 0, 0:SUB] if False else AakT[0:SUB, 0, 0:SUB], R=[AakT.d])
                        dbg("TT", Pm[0:SUB, 0, 0:SUB], R=[Pm.d]); dbg("ArkT", ArkT[0:SUB, 0, 0:SUB], R=[ArkT.d])
                    for ci in range(CPS):
                        r0 = 0; rsl = slice(0, CH); cols = slice(q0, q0 + CH)
                        gch = q0 // CH
                        bX = bank()
                        for c in range(4):
                            K.mm(bX.t[rsl, c * 128:(c + 1) * 128], ah[:, c, cols], STb[:, c, seg, :, :], start=True, stop=False, R=[ah.ds[c], STb.d], W=[bX.d])
                            for h in (2 * c, 2 * c + 1):
                                hs = slice(h * 64, (h + 1) * 64)
                                K.mm(bX.t[rsl, hs], AakT[rsl, h, rsl], vT[rsl, hs], start=False, stop=(h == 2 * c + 1), R=[AakT.ds[h // 4], vT.d], W=[bX.d])
                        K.cp("act", Xs[rsl, :], bX.t[rsl, :], W=[bX.d, Xs.d]); rel(bX)
                        bU = bank()
                        for h in range(8):
                            hs = slice(h * 64, (h + 1) * 64)
                            K.mm(bU.t[rsl, hs], Pm[rsl, h, rsl], Xs[rsl, hs], R=[Pm.ds[h // 4], Xs.d], W=[bU.d])
                        K.cp("dve", Us[rsl, :], bU.t[rsl, :], W=[bU.d, Us.d]); rel(bU)
                        for c in range(4):
                            K.mm(yb.t[rsl, c * 128:(c + 1) * 128], rh[:, c, cols], STb[:, c, seg, :, :], start=True, stop=False, R=[rh.ds[c], STb.d], W=[yb.d])
                            for h in (2 * c, 2 * c + 1):
                                hs = slice(h * 64, (h + 1) * 64)
                                K.mm(yb.t[rsl, hs], ArbT[rsl, h, rsl], Us[rsl, hs], start=False, stop=False, R=[ArbT.ds[h // 4], Us.d], W=[yb.d])
                                K.mm(yb.t[rsl, hs], ArkT[rsl, h, rsl], vT[rsl, hs], start=False, stop=(h == 2 * c + 1), R=[ArkT.ds[h // 4], vT.d], W=[yb.d])
                        bS = bank()
                        for h in range(8):
                            c = h // 2; p0 = 64 * (h % 2); hs = slice(h * 64, (h + 1) * 64)
                            K.mm(bS.t[p0:p0 + 64, c * 64:(c + 1) * 64], btT[rsl, hs], Us[rsl, hs], start=True, stop=False, R=[btT.d, Us.d], W=[bS.d])
                            K.mm(bS.t[p0:p0 + 64, c * 64:(c + 1) * 64], ktT[rsl, hs], vT[rsl, hs], start=False, stop=True, R=[ktT.d, vT.d], W=[bS.d])
                        for c in range(4):
                            K.stt(ST[:, c, seg, :], ST[:, c, seg, :], WC[:, c, gch:gch + 1], bS.t[:, c * 64:(c + 1) * 64], ALU.mult, ALU.add,
                                  R=[WC.ds[c]], W=[bS.d, ST.ds[c]])
                        rel(bS)
                        K.cp("act", STb[0:64, :, seg, 0, :], ST[0:64, :, seg, :], R=ST.ds, W=[STb.d])
                        K.cp("act", STb[64:128, :, seg, 1, :], ST[64:128, :, seg, :], R=ST.ds, W=[STb.d])
                    if l == 0 and q == 0:
                        dbg("Xs", Xs[0:SUB, :], R=[Xs.d]); dbg("Us", Us[0:SUB, :], R=[Us.d]); dbg("Y", yb.t[0:SUB, :], R=[yb.d])
                        dbg("ST", ST[:, :, 0, :], R=ST.ds)
                    Y3 = yb.t[0:SUB, :].rearrange("p (h v) -> p h v", v=64)
                    sm, sq_, mu_, rs_ = st8
                    K.op("dve", lambda e: e.tensor_reduce(out=sm[0:SUB, :], in_=Y3, op=ALU.add, axis=AX.X), W=[yb.d, sm.d])
                    K.act(ysq[0:SUB, :], yb.t[0:SUB, :], AF.Square, W=[yb.d, ysq.d])
                    K.op("dve", lambda e: e.tensor_reduce(out=sq_[0:SUB, :], in_=ysq[0:SUB, :].rearrange("p (h v) -> p h v", v=64), op=ALU.add, axis=AX.X),
                         R=[ysq.d], W=[sq_.d])
                    K.ts("dve", mu_[0:SUB, :], sm[0:SUB, :], 1.0 / HD, None, ALU.mult, R=[sm.d], W=[mu_.d])
                    K.tt("dve", sm[0:SUB, :], mu_[0:SUB, :], mu_[0:SUB, :], ALU.mult, R=[mu_.d], W=[sm.d])
                    K.stt(rs_[0:SUB, :], sq_[0:SUB, :], 1.0 / HD, sm[0:SUB, :], ALU.mult, ALU.subtract, R=[sq_.d, sm.d], W=[rs_.d])
                    K.act(rs_[0:SUB, :], rs_[0:SUB, :], AF.Ln, bias=GN_EPS, W=[rs_.d])
                    K.act(rs_[0:SUB, :], rs_[0:SUB, :], AF.Exp, scale=-0.5, W=[rs_.d])
                    K.tt("dve", ysq[0:SUB, :].rearrange("p (h v) -> p h v", v=64), Y3, mu_[0:SUB, :].unsqueeze(2).broadcast_to([SUB, 8, 64]), ALU.subtract,
                         R=[mu_.d], W=[yb.d, ysq.d])
                    K.tt("dve", yn[0:SUB, :].rearrange("p (h v) -> p h v", v=64), ysq[0:SUB, :].rearrange("p (h v) -> p h v", v=64),
                         rs_[0:SUB, :].unsqueeze(2).broadcast_to([SUB, 8, 64]), ALU.mult, R=[rs_.d, ysq.d], W=[yn.d])
                    b = bank(); pb = b.t[:, :].bitcast(BF16)
                    for c in range(4):
                        K.tr(pb[:, c * 128:c * 128 + SUB], yn[0:SUB, c * 128:(c + 1) * 128], ident[0:SUB, 0:SUB], R=[yn.d, cst], W=[b.d])
                    for c in range(4):
                        oc = o1s[c]
                        K.stt(oc[:, 0:SUB], pb[:, c * 128:c * 128 + SUB], vc("lnx_g", c), bv[:, c, qs], ALU.mult, ALU.add, R=[V.d, bv.ds[c]], W=[b.d, oc.d])
                    for c in range(4):
                        oc = o1s[c]
                        K.stt(mixT[:, 4 + c, qs], oc[:, 0:SUB], vc("lnx_b", c), g4[:, c, qs], ALU.add, ALU.mult, R=[V.d, oc.d, g4.ds[c]], W=[mixT.ds[4 + c]])
                    rel(b)

                if NSUB == 1:
                    core_body(0, bufsets[0], banks[5])
                else:
                    for q in range(NSUB // 2):
                        recs = []
                        for si, rn in ((0, "A"), (1, "B")):
                            cur_ring[0] = rn
                            recs.append(K.record(lambda: core_body(si * (NSUB // 2) + q, bufsets[si], banks[5 + si])))
                        cur_ring[0] = "all"
                        K.replay(*recs)
                if l == 0: dbg("mixr", mixT[:, 4:8, 0:W], R=mixT.ds)

        def outproj_phase(S, l, which):
            pass

        def layer_pass(S, l, last):
            W = S.W
            conv_phase(S, l, last)
            rwkv_phase(S, l, last)
            with contextlib.ExitStack() as ph:
                K.barrier()
                m = sb([128, 8, 512], es=ph, nd=8)
                for j in range(8):
                    b = proj(S, next_slab(("w_out", l, j)), mixT)
                    K.cp("act", m[:, j, 0:W], b.t[:, 0:W], W=[b.d, m.ds[j]])
                    K.act(phase_bufs(ph)[:, j, 0:W], b.t[:, 0:W], AF.Square, W=[b.d, ph.sq.ds[j]]); rel(b)
                if l == 0: dbg("m1", m[:, :, 0:W], R=m.ds)
                post_norm(S, l, m, 2, ph)
                if l == 0: dbg("x1", x[:, :, 0:W], R=x.ds)
                norm_mod(S, l, 4, 3, ph)
                act = sb([128, 22, 512], BF16, es=ph, nd=22)
                tg = [sb([128, 512], es=ph) for _ in range(2)]; tu = [sb([128, 512], es=ph) for _ in range(2)]
                for j in range(22):
                    t = tg[j % 2]; u = tu[j % 2]
                    bg = proj(S, next_slab(("w_gate", l, j)), hT)
                    K.act(t[:, 0:W], bg.t[:, 0:W], AF.Tanh, scale=0.5, W=[bg.d, t.d])
                    K.stt(u[:, 0:W], t[:, 0:W], 1.0, bg.t[:, 0:W], ALU.add, ALU.mult, R=[t.d], W=[bg.d, u.d]); rel(bg)
                    bu = proj(S, next_slab(("w_up", l, j)), hT)
                    K.stt(act[:, j, 0:W], u[:, 0:W], 0.5, bu.t[:, 0:W], ALU.mult, ALU.mult, R=[u.d], W=[bu.d, act.ds[j]]); rel(bu)
                for j in range(8):
                    b = proj(S, next_dslab(("w_down", l, j)), act, nk=22)
                    K.cp("act", m[:, j, 0:W], b.t[:, 0:W], W=[b.d, m.ds[j]])
                    K.act(phase_bufs(ph)[:, j, 0:W], b.t[:, 0:W], AF.Square, W=[b.d, ph.sq.ds[j]]); rel(b)
                if l == 0: dbg("act", act[:, 0:4, 0:W], R=act.ds); dbg("m2", m[:, :, 0:W], R=m.ds)
                post_norm(S, l, m, 5, ph)
                if l == 0: dbg("x2", x[:, :, 0:W], R=x.ds)

        def store_states(S, l):
            for si, (c0, n, s) in enumerate(S.segs):
                b = bank()
                for c in range(4):
                    K.tr(b.t[0:CB, c * 128:(c + 1) * 128], S.gl[:, c, si, :], identf[:, :], R=[S.gl.d, cst], W=[b.d])
                K.cp("dve", stage[0:CB, 0:512], b.t[0:CB, :], W=[b.d, stage.d]); rel(b)
                K.dma("sp", q_stage, o_conv[l, s], stage[0:CB, 0:512], R=[stage.d])
                b = bank()
                K.tr(b.t[0:14, 0:128], S.zl[:, :, si], identf[:, :], R=[S.zl.d, cst], W=[b.d])
                K.cp("dve", stage[0:14, 0:128], b.t[0:14, 0:128], W=[b.d, stage.d]); rel(b)
                K.dma("sp", q_stage, o_shift[l, s], stage[0:14, 0:128], R=[stage.d])
                b = bank()
                for c in range(4):
                    K.tr(b.t[0:64, c * 128:(c + 1) * 128], S.ST[l][:, c, si, :], identf[:, :], R=[S.ST[l].ds[c], cst], W=[b.d])
                K.cp("dve", stage[0:64, 0:512], b.t[0:64, :], W=[b.d, stage.d]); rel(b)
                K.dma("sp", q_stage, o_wkv[l, s].rearrange("h v k -> v h k"), stage[0:64, 0:512].rearrange("v (h k) -> v h k", k=64), R=[stage.d])

        for p in passes:
            S = SS if p == "s" else PS
            dbg_on[0] = (p == "s")
            last = (p == "s") or (p == NPASS - 1)
            load_x(S, 0 if p == "s" else p)
            for l in range(L):
                layer_pass(S, l, last)
                if last: store_states(S, l)
            store_x(S, 0 if p == "s" else p)
        K.barrier()
    return nc


WIN_ORDER = []
for _c in range(4): WIN_ORDER += [4 + _c, _c]
WIN_ORDER += [20, 21]
for _c in (0, 2): WIN_ORDER += [12 + _c, 13 + _c, 8 + _c, 9 + _c, 16 + _c, 17 + _c]


def _slabify(w):
    Lw, Kd, Nd = w.shape
    return np.ascontiguousarray(w.reshape(Lw, Kd // 128, 128, Nd // 128, 128).transpose(0, 3, 2, 1, 4).reshape(Lw, Nd // 128, 128, Kd))


def _colmajor(v):
    Lw, n = v.shape
    return v.reshape(Lw, n // 128, 128).transpose(0, 2, 1)


_NC_CACHE = {}


def kernel(x_prompt, x_sample, cache_conv, state_shift, state_wkv, c_prompt, c_sample, w_mod, b_mod, g_mix_pre, g_mix_post,
           g_ffn_pre, g_ffn_post, w_in, conv_dw, conv_b, conv_ln_g, conv_ln_b, mu_shift, w0, w2, a0, a2, g2, k_k, k_a, r_k,
           lnx_g, lnx_b, w_out, w_gate, w_up, w_down):
    f = lambda a: np.ascontiguousarray(np.asarray(a, dtype=np.float32))
    x_prompt = f(x_prompt); SEQ = x_prompt.shape[1]; B = x_prompt.shape[0]
    nco = B // 2
    if SEQ not in _NC_CACHE:
        _NC_CACHE[SEQ] = build(SEQ)
    nc = _NC_CACHE[SEQ]
    parts = [_colmajor(f(v)) for v in (g_mix_pre, g_mix_post, g_ffn_pre, g_ffn_post, b_mod, conv_b, conv_ln_g, conv_ln_b, mu_shift,
                                       w0, a0, k_k, k_a, f(r_k).reshape(L, DR), lnx_g, lnx_b)]
    cdw = f(conv_dw).reshape(L, CW, 4, 128).transpose(0, 3, 2, 1).reshape(L, 128, 4 * CW)
    vecs = np.ascontiguousarray(np.concatenate(parts + [cdw], axis=2))
    assert vecs.shape[2] == NV_IN
    shared = {
        "wmod": f(w_mod), "vecs": vecs, "wa2": np.ascontiguousarray(np.concatenate([f(w2), f(a2)], axis=1)), "g2": f(g2),
        "w_in": _slabify(f(w_in)), "w_out": _slabify(f(w_out)), "w_gate": _slabify(f(w_gate)), "w_up": _slabify(f(w_up)),
        "w_down": _slabify(f(w_down)),
    }
    x_sample = f(x_sample); cache_conv = f(cache_conv); state_shift = f(state_shift); state_wkv = f(state_wkv)
    c_prompt = f(c_prompt); c_sample = f(c_sample)
    in_maps = []
    for i in range(nco):
        call = np.stack([c_prompt[2 * i], c_prompt[2 * i + 1], c_sample[i]], axis=0)
        m = dict(shared)
        m.update({
            "xp": np.ascontiguousarray(x_prompt[2 * i:2 * i + 2]), "xs": np.ascontiguousarray(x_sample[i]),
            "cT": np.ascontiguousarray(call.reshape(3, 8, 128).transpose(2, 1, 0)),
            "cconv": np.ascontiguousarray(cache_conv[:, i]), "sshift": np.ascontiguousarray(state_shift[:, i].reshape(L, 14, 128)),
            "swkv": np.ascontiguousarray(state_wkv[:, i]),
        })
        in_maps.append(m)
    res = run_bass_kernel_spmd(nc, in_maps, core_ids=list(range(nco))).results
    if DEBUG:
        global LAST_RES
        LAST_RES = res
    yp = np.concatenate([r["yp"] for r in res], axis=0)
    ys = np.stack([r["ys"] for r in res], axis=0)
    oc = np.stack([r["o_conv"] for r in res], axis=0)
    osh = np.stack([r["o_shift"] for r in res], axis=0).reshape(nco, L, 3, DSH)
    ow = np.stack([r["o_wkv"] for r in res], axis=0)
    pr = lambda a: np.ascontiguousarray(np.moveaxis(a[:, :, 0:2], 1, 0).reshape((L, 2 * nco) + a.shape[3:]))
    sm = lambda a: np.ascontiguousarray(np.moveaxis(a[:, :, 2], 1, 0))
    return (yp, ys, pr(oc), pr(osh), pr(ow), sm(oc), sm(osh), sm(ow))
```
